# Optimizing a Trainium2 kernel written in Bass

```python
import jax, jax.numpy as jnp
from jax import lax
import numpy as np

D_MODEL = 2048
BATCH = 4
SEQ = 2048
DEPTH = 2

D_MIX = D_MODEL
EPS = 1e-6
SWA_HEAD_DIM = 64
SWA_HEADS = D_MIX // 2 // SWA_HEAD_DIM
SWA_KV_HEADS = 4
SWA_GROUP = SWA_HEADS // SWA_KV_HEADS
SWA_WIDTH = SWA_HEADS * SWA_HEAD_DIM
SWA_KV_WIDTH = SWA_KV_HEADS * SWA_HEAD_DIM
WINDOW = 128
ROT_DIM = SWA_HEAD_DIM // 4
ROPE_THETA = 500000.0
SG_WIDTH = D_MIX // 4
SG_GROUPS = 8
SG_GROUP_DIM = SG_WIDTH // SG_GROUPS
SG_CHUNK = 128
GLA_HEADS = 4
GLA_WIDTH = D_MIX // 4
GLA_DV = GLA_WIDTH // GLA_HEADS
GLA_DK = GLA_DV // 2
GLA_KEY_WIDTH = GLA_HEADS * GLA_DK
GLA_GATE_RANK = 16
GLA_GATE_TAU = 16.0
GLA_CHUNK = 64
IN_SIZES = (SWA_WIDTH, SWA_KV_WIDTH, SWA_KV_WIDTH,
            SG_WIDTH, SG_WIDTH,
            GLA_KEY_WIDTH, GLA_KEY_WIDTH, GLA_WIDTH,
            GLA_GATE_RANK,
            D_MIX)
IN_PROJ_WIDTH = SWA_WIDTH + 2 * SWA_KV_WIDTH + 2 * SG_WIDTH + 2 * GLA_KEY_WIDTH + GLA_WIDTH + GLA_GATE_RANK + D_MIX

kernel_name = 'hybrid_swa_sgmlp_gla_parallel_heads'


def rms_norm(x, g):
    x32 = x.astype(jnp.float32)
    y = x32 * lax.rsqrt(jnp.mean(x32 * x32, axis=-1, keepdims=True) + EPS)
    return (y * g.astype(jnp.float32)).astype(x.dtype)


def partial_rope(t, positions):
    half = ROT_DIM // 2
    inv_freq = ROPE_THETA ** (-(jnp.arange(half, dtype=jnp.float32) * (2.0 / ROT_DIM)))
    ang = positions.astype(jnp.float32)[..., None] * inv_freq
    cos = jnp.cos(ang)[:, :, None, :]
    sin = jnp.sin(ang)[:, :, None, :]
    tr = t[..., :ROT_DIM].astype(jnp.float32)
    t1, t2 = tr[..., :half], tr[..., half:]
    rot = jnp.concatenate([t1 * cos - t2 * sin, t2 * cos + t1 * sin], axis=-1)
    return jnp.concatenate([rot.astype(t.dtype), t[..., ROT_DIM:]], axis=-1)


def sliding_window_attention(q, k, v, sinks):
    bsz, seq = q.shape[0], q.shape[1]
    nb = seq // WINDOW
    qb = q.reshape(bsz, nb, WINDOW, SWA_KV_HEADS, SWA_GROUP, SWA_HEAD_DIM).astype(jnp.float32)

    def band(t):
        tb = t.reshape(bsz, nb, WINDOW, SWA_KV_HEADS, SWA_HEAD_DIM)
        prev = jnp.pad(tb[:, :-1], ((0, 0), (1, 0), (0, 0), (0, 0), (0, 0)))
        return jnp.concatenate([prev, tb], axis=2).astype(jnp.float32)

    kb, vb = band(k), band(v)
    scores = jnp.einsum('bnqgrd,bnkgd->bngrqk', qb, kb) * (SWA_HEAD_DIM ** -0.5)
    qi = jnp.arange(WINDOW)[:, None]
    kj = jnp.arange(2 * WINDOW)[None, :]
    dist = qi + WINDOW - kj
    blk = jnp.arange(nb)[:, None, None]
    valid = (dist >= 0) & (dist < WINDOW) & (blk * WINDOW + kj[None] - WINDOW >= 0)
    scores = jnp.where(valid[None, :, None, None], scores, -jnp.inf)
    sink = sinks.astype(jnp.float32).reshape(1, 1, SWA_KV_HEADS, SWA_GROUP, 1, 1)
    m = jnp.maximum(scores.max(axis=-1, keepdims=True), sink)
    p = jnp.exp(scores - m)
    probs = p / (p.sum(axis=-1, keepdims=True) + jnp.exp(sink - m))
    out = jnp.einsum('bngrqk,bnkgd->bnqgrd', probs, vb)
    return out.reshape(bsz, seq, SWA_WIDTH).astype(q.dtype)


def chunked_spatial_gating(u, v, w_s, b_s, ln_g, ln_b):
    bsz, seq = v.shape[0], v.shape[1]
    nc = seq // SG_CHUNK
    v32 = v.astype(jnp.float32)
    mu = jnp.mean(v32, axis=-1, keepdims=True)
    var = jnp.mean(jnp.square(v32 - mu), axis=-1, keepdims=True)
    vn = (v32 - mu) * lax.rsqrt(var + EPS) * ln_g.astype(jnp.float32) + ln_b.astype(jnp.float32)
    vn = vn.reshape(bsz, nc, SG_CHUNK, SG_GROUPS, SG_GROUP_DIM)
    causal = jnp.tril(jnp.ones((SG_CHUNK, SG_CHUNK), dtype=bool))
    w = jnp.where(causal[None], w_s.astype(jnp.float32), 0.0)
    mixed = jnp.einsum('gts,bnsgc->bntgc', w, vn) + b_s.astype(jnp.float32).T[None, None, :, :, None]
    return (u.astype(jnp.float32) * mixed.reshape(bsz, seq, SG_WIDTH)).astype(u.dtype)


def gated_linear_attention(q, k, v, log_alpha):
    bsz, seq = q.shape[0], q.shape[1]
    nc = seq // GLA_CHUNK

    def to_chunks(t):
        return t.astype(jnp.float32).reshape(bsz, nc, GLA_CHUNK, GLA_HEADS, t.shape[-1]).transpose(1, 0, 3, 2, 4)

    causal = jnp.tril(jnp.ones((GLA_CHUNK, GLA_CHUNK), dtype=bool))

    def step(state, xs):
        qc, kc, vc, lac = xs
        b = jnp.cumsum(lac, axis=2)
        diff = b[:, :, :, None, :] - b[:, :, None, :, :]
        decay = jnp.exp(jnp.where(causal[:, :, None], diff, -jnp.inf))
        scores = jnp.einsum('bhtd,bhsd,bhtsd->bhts', qc, kc, decay)
        o = jnp.einsum('bhts,bhsv->bhtv', scores, vc) + jnp.einsum('bhtd,bhdv->bhtv', qc * jnp.exp(b), state)
        b_last = b[:, :, -1:, :]
        state = state * jnp.exp(b_last)[:, :, 0, :, None] + jnp.einsum('bhsd,bhsv->bhdv', kc * jnp.exp(b_last - b), vc)
        return state, o

    state0 = jnp.zeros((bsz, GLA_HEADS, GLA_DK, GLA_DV), dtype=jnp.float32)
    qs = to_chunks(q) * (GLA_DK ** -0.5)
    _, o = lax.scan(step, state0, (qs, to_chunks(k), to_chunks(v), to_chunks(log_alpha)))
    return o.transpose(1, 0, 3, 2, 4).reshape(bsz, seq, GLA_HEADS, GLA_DV)


def hybrid_layer(x, c, positions, w_mod, b_mod, g_pre, g_post, w_in, w_out, swa_sinks,
                 sg_w, sg_b, sg_ln_g, sg_ln_b, gla_w_gate_up, gla_b_gate, gla_norm_g):
    bsz, seq = x.shape[0], x.shape[1]
    mod = jax.nn.silu(c) @ w_mod + b_mod
    shift, scale, gate = jnp.split(mod, 3, axis=-1)
    h = rms_norm(x, g_pre) * (1.0 + scale[:, None, :]) + shift[:, None, :]
    proj = h @ w_in
    offsets = np.cumsum(IN_SIZES)[:-1].tolist()
    a_q, a_k, a_v, s_u, s_v, c_q, c_k, c_v, c_g, z = jnp.split(proj, offsets, axis=-1)
    a_q = partial_rope(a_q.reshape(bsz, seq, SWA_HEADS, SWA_HEAD_DIM), positions)
    a_k = partial_rope(a_k.reshape(bsz, seq, SWA_KV_HEADS, SWA_HEAD_DIM), positions)
    a_v = a_v.reshape(bsz, seq, SWA_KV_HEADS, SWA_HEAD_DIM)
    y_a = sliding_window_attention(a_q, a_k, a_v, swa_sinks)
    y_b = chunked_spatial_gating(jax.nn.gelu(s_u), jax.nn.gelu(s_v), sg_w, sg_b, sg_ln_g, sg_ln_b)
    gate_logits = (c_g @ gla_w_gate_up + gla_b_gate).astype(jnp.float32)
    log_alpha = jax.nn.log_sigmoid(gate_logits) / GLA_GATE_TAU
    o_c = gated_linear_attention(c_q.reshape(bsz, seq, GLA_HEADS, GLA_DK),
                                 c_k.reshape(bsz, seq, GLA_HEADS, GLA_DK),
                                 c_v.reshape(bsz, seq, GLA_HEADS, GLA_DV),
                                 log_alpha.reshape(bsz, seq, GLA_HEADS, GLA_DK))
    y_c = rms_norm(o_c, gla_norm_g).reshape(bsz, seq, GLA_WIDTH).astype(x.dtype)
    y = jnp.concatenate([y_a, y_b, y_c], axis=-1) * jax.nn.silu(z)
    out = y @ w_out
    return x + gate[:, None, :] * rms_norm(out, g_post)


def setup_inputs(seed: int = 0) -> dict:
    key = jax.random.key(seed)
    ks = jax.random.split(key, 17)
    nrm = jax.random.normal
    f32 = jnp.float32
    x = nrm(ks[0], (BATCH, SEQ, D_MODEL), f32)
    c = nrm(ks[1], (BATCH, D_MODEL), f32)
    positions = jax.random.randint(ks[2], (BATCH, 1), 0, 4096, dtype=jnp.int32) + jnp.arange(SEQ, dtype=jnp.int32)[None, :]
    w_mod = nrm(ks[3], (DEPTH, D_MODEL, 3 * D_MODEL), f32) * (0.5 * D_MODEL ** -0.5)
    b_mod = 0.01 * nrm(ks[4], (DEPTH, 3 * D_MODEL), f32)
    g_pre = 1.0 + 0.05 * nrm(ks[5], (DEPTH, D_MODEL), f32)
    g_post = 1.0 + 0.05 * nrm(ks[6], (DEPTH, D_MODEL), f32)
    w_in = nrm(ks[7], (DEPTH, D_MODEL, IN_PROJ_WIDTH), f32) * (D_MODEL ** -0.5)
    w_out = nrm(ks[8], (DEPTH, D_MIX, D_MODEL), f32) * (D_MIX ** -0.5)
    swa_sinks = 0.5 * nrm(ks[9], (DEPTH, SWA_HEADS), f32)
    sg_w = nrm(ks[10], (DEPTH, SG_GROUPS, SG_CHUNK, SG_CHUNK), f32) * (SG_CHUNK ** -0.5)
    sg_b = 1.0 + 0.1 * nrm(ks[11], (DEPTH, SG_GROUPS, SG_CHUNK), f32)
    sg_ln_g = 1.0 + 0.05 * nrm(ks[12], (DEPTH, SG_WIDTH), f32)
    sg_ln_b = 0.02 * nrm(ks[13], (DEPTH, SG_WIDTH), f32)
    gla_w_gate_up = nrm(ks[14], (DEPTH, GLA_GATE_RANK, GLA_KEY_WIDTH), f32) * (GLA_GATE_RANK ** -0.5)
    gla_b_gate = 0.1 * nrm(ks[15], (DEPTH, GLA_KEY_WIDTH), f32)
    gla_norm_g = 1.0 + 0.05 * nrm(ks[16], (DEPTH, GLA_DV), f32)
    return {'x': x, 'c': c, 'positions': positions, 'w_mod': w_mod, 'b_mod': b_mod,
            'g_pre': g_pre, 'g_post': g_post, 'w_in': w_in, 'w_out': w_out, 'swa_sinks': swa_sinks,
            'sg_w': sg_w, 'sg_b': sg_b, 'sg_ln_g': sg_ln_g, 'sg_ln_b': sg_ln_b,
            'gla_w_gate_up': gla_w_gate_up, 'gla_b_gate': gla_b_gate, 'gla_norm_g': gla_norm_g}


def reference(x, c, positions, w_mod, b_mod, g_pre, g_post, w_in, w_out, swa_sinks,
              sg_w, sg_b, sg_ln_g, sg_ln_b, gla_w_gate_up, gla_b_gate, gla_norm_g):
    for l in range(DEPTH):
        x = hybrid_layer(x, c, positions, w_mod[l], b_mod[l], g_pre[l], g_post[l], w_in[l], w_out[l],
                         swa_sinks[l], sg_w[l], sg_b[l], sg_ln_g[l], sg_ln_b[l],
                         gla_w_gate_up[l], gla_b_gate[l], gla_norm_g[l])
    return x
```

```python
import contextlib
import math
import numpy as np
import concourse.bass as bass
import concourse.mybir as mybir
from concourse.bass_utils import run_bass_kernel_spmd

F32 = mybir.dt.float32
BF16 = mybir.dt.bfloat16
I32 = mybir.dt.int32
AF = mybir.ActivationFunctionType
ALU = mybir.AluOpType
AX = mybir.AxisListType

NT = 8
KC = 16
D = 2048
EPS = 1e-6
NWB = 2
PI = math.pi

Q_ORDER_A = [0, 2, 1, 3, 4, 6, 5, 7]
Q_ORDER_B = [8, 10, 9, 11, 12, 14, 13, 15]
CHUNKS = ["cv", "cqk", "zc", "za0", "za1", "kv", "qA", "qB", "zb", "vs", "u"]
Z_FC0 = {"za0": 0, "za1": 4, "zb": 8, "zc": 12}


class Prog:
    ENGS = ("pe", "act", "dve", "pool", "sp")

    def __init__(self, nc, stack):
        self.nc = nc
        self.stack = stack
        self.q = {e: [] for e in self.ENGS}
        self.sems = {}
        self.cnt = {}
        self.seen = {e: {} for e in self.ENGS}
        for e in ("pe", "act", "dve", "pool"):
            self.newsem(e)

    def newsem(self, key):
        self.sems[key] = self.stack.enter_context(self.nc.semaphore("s_" + key))
        self.cnt[key] = 0
        return key

    def _flat(self, waits, need):
        for t in waits:
            if t is None:
                continue
            if isinstance(t, (list, tuple)) and not (len(t) == 2 and isinstance(t[0], str)):
                self._flat(t, need)
            elif isinstance(t, dict):
                for k, v in t.items():
                    need[k] = max(need.get(k, 0), v)
            else:
                k, v = t
                need[k] = max(need.get(k, 0), v)

    def _waits(self, eng, waits):
        need = {}
        self._flat(waits, need)
        out = []
        for k, v in need.items():
            if self.seen[eng].get(k, 0) < v:
                self.seen[eng][k] = v
                out.append((k, v))
        return out

    def op(self, eng, fn, waits=(), inc=True):
        w = self._waits(eng, waits)
        ticket = None
        if inc:
            self.cnt[eng] += 1
            ticket = (eng, self.cnt[eng])
        self.q[eng].append((fn, w, eng if inc else None, 1))
        return ticket

    def next_ticket(self, eng):
        return (eng, self.cnt[eng] + 1)

    def dma(self, eng, semkey, out, in_, waits=()):
        if semkey not in self.sems:
            self.newsem(semkey)
        w = self._waits(eng, waits)
        self.cnt[semkey] += 16
        ticket = (semkey, self.cnt[semkey])
        self.q[eng].append((lambda e: e.dma_start(out=out, in_=in_), w, semkey, 16))
        return ticket

    def wait_only(self, eng, waits):
        w = self._waits(eng, waits)
        if w:
            self.q[eng].append((None, w, None, 0))

    def snapshot(self):
        return [(e, self.cnt[e]) for e in ("pe", "act", "dve", "pool") if self.cnt[e] > 0]

    def emit(self, block):
        sems = self.sems

        def run(engname):
            def body(e):
                for fn, w, inckey, incval in self.q[engname]:
                    for k, v in w:
                        e.wait_ge(sems[k], v)
                    if fn is None:
                        continue
                    ins = fn(e)
                    if inckey is not None:
                        ins.then_inc(sems[inckey], incval)
            return body

        block.tensor(run("pe"))
        block.scalar(run("act"))
        block.vector(run("dve"))
        block.gpsimd(run("pool"))
        block.sync(run("sp"))


class Buf:
    def __init__(self, ap):
        self.ap = ap
        self.w = None
        self.r = {}

    def rd(self):
        return [self.w]

    def wr(self):
        return [self.w, dict(self.r)]

    def did_read(self, t):
        if t is None:
            return
        k, v = t
        self.r[k] = max(self.r.get(k, 0), v)

    def did_write(self, t):
        self.w = t
        self.r = {}


def dtsize(dt):
    return 4 if dt in (F32, I32) else 2


class Arena:
    def __init__(self, ap, nbytes):
        self.ap = ap
        self.nbytes = nbytes
        self.off = 0

    def at(self, off, shape, dt, parts=128):
        n = 1
        for s in shape:
            n *= s
        sz = n * dtsize(dt)
        assert off % 4 == 0 and off + sz <= self.nbytes, (off, sz, self.nbytes)
        v = self.ap[0:parts, off // 2:(off + sz) // 2]
        if dt != BF16:
            v = v.bitcast(dt)
        if len(shape) == 2:
            v = v.rearrange("p (a b) -> p a b", a=shape[0])
        elif len(shape) == 3:
            v = v.rearrange("p (a b c) -> p a b c", a=shape[0], b=shape[1])
        elif len(shape) == 4:
            v = v.rearrange("p (a b c d) -> p a b c d", a=shape[0], b=shape[1], c=shape[2])
        return v

    def alloc(self, shape, dt, parts=128):
        n = 1
        for s in shape:
            n *= s
        sz = (n * dtsize(dt) + 31) // 32 * 32
        off = self.off
        self.off += sz
        assert self.off <= self.nbytes, ("arena overflow", self.off, self.nbytes)
        return self.at(off, shape, dt, parts)


class Sub:
    def __init__(self, arena, base, size):
        self.arena, self.base, self.size, self.off = arena, base, size, 0

    def alloc(self, shape, dt, parts=128):
        n = 1
        for s in shape:
            n *= s
        sz = (n * dtsize(dt) + 31) // 32 * 32
        off = self.base + self.off
        self.off += sz
        assert self.off <= self.size, ("region overflow", self.off, self.size)
        return self.arena.at(off, shape, dt, parts)


class BankPool:
    def __init__(self, bufs):
        self.free = list(bufs)

    def put(self, *bs):
        for b in bs:
            self.free.append(b)


def take(reqs):
    while True:
        if all(len(p.free) >= n for p, n in reqs):
            out = []
            for p, n in reqs:
                for _ in range(n):
                    out.append(p.free.pop(0))
            return out
        yield


class StopBuild(Exception):
    pass


def build(n_layers=2, dbg=False, stop=None):
    nc = bass.Bass("TRN2", target_bir_lowering=False)
    L = n_layers

    def din(name, shape, dt=F32):
        return nc.dram_tensor(name, shape, dt, kind="ExternalInput").ap()

    x_d = din("x", [2048, D])
    pos_d = din("pos", [128, 16], I32)
    cT_d = din("cT", [128, 16])
    wmod_d = din("wmod", [24, 128, KC * 512])
    bmod_d = din("bmod", [2, 1, 6144])
    gpreT_d = din("gpreT", [128, 32])
    gpost_d = din("gpost", [128, 2 * D])
    import os as _os0
    TINY = _os0.environ.get("TINY") == "1"
    win_d = din("win", [2 if TINY else 2 * 11, 128, KC * 512])
    wing_d = din("wing", [2, 128, KC * 16])
    wout_d = din("wout", [1 if TINY else 2 * 4, 128, KC * 512])
    sinks_d = din("sinks", [128, 32])
    sgw_d = din("sgw", [2, 128, 1024])
    sgb_d = din("sgb", [128, 16])
    lng_d = din("lng", [128, 1024])
    lnb_d = din("lnb", [128, 1024])
    wup_d = din("wup", [17, 512])
    gng_d = din("gng", [128, 256])
    cst_d = din("cst", [128, 640])
    cst2_d = din("cst2", [128, 16])
    y_d = nc.dram_tensor("y", [2048, D], F32, kind="ExternalOutput").ap()
    x1_d = nc.dram_tensor("x1s", [2048, D], F32).ap()
    dbg_t = []

    with contextlib.ExitStack() as st:
        P = Prog(nc, st)
        ARENA_BYTES = 212800
        arena_t = st.enter_context(nc.sbuf_tensor("arena", [128, ARENA_BYTES // 2], BF16))
        A = Arena(arena_t, ARENA_BYTES)
        ps_t = st.enter_context(nc.psum_tensor("ps", [128, 4096], F32))
        block = st.enter_context(nc.Block())

        def bank(i):
            return ps_t[:, i * 512:(i + 1) * 512]

        acc_b = [Buf(bank(0)), Buf(bank(1))]
        psb_ = []
        for k in range(6):
            b_ = Buf(bank(2 + k))
            b_.bf = bank(2 + k).bitcast(BF16)[:, 0:512]
            b_.bff = bank(2 + k).bitcast(BF16)
            psb_.append(b_)
        mixp = BankPool(psb_)
        tbp = mixp
        rr = {"acc": 0}

        def next_acc():
            rr["acc"] += 1
            return acc_b[rr["acc"] % 2]

        def op(eng, fn, reads=(), writes=(), waits=()):
            w = [b.rd() for b in reads] + [b.wr() for b in writes] + list(waits)
            t = P.op(eng, fn, w)
            for b in reads:
                b.did_read(t)
            for b in writes:
                b.did_write(t)
            return t

        def pe_group(items, reads=(), writes=(), waits=()):
            w = [b.rd() for b in reads] + [b.wr() for b in writes] + list(waits)
            n = len(items)
            t = None
            for i, fn in enumerate(items):
                last = i == n - 1
                t_ = P.op("pe", fn, w if i == 0 else (), inc=last)
                if last:
                    t = t_
            for b in reads:
                b.did_read(t)
            for b in writes:
                b.did_write(t)
            return t

        def dma(eng, key, out, in_, reads=(), writes=(), waits=()):
            w = [b.rd() for b in reads] + [b.wr() for b in writes] + list(waits)
            t = P.dma(eng, key, out, in_, w)
            for b in reads:
                b.did_read(t)
            for b in writes:
                b.did_write(t)
            return t

        def act(out, in_, func, reads=(), writes=(), waits=(), **kw):
            return op("act", lambda e: e.activation(out=out, in_=in_, func=func, **kw), reads, writes, waits)

        def tt(eng, out, in0, in1, alu, reads=(), writes=(), waits=()):
            return op(eng, lambda e: e.tensor_tensor(out=out, in0=in0, in1=in1, op=alu), reads, writes, waits)

        def ts(eng, out, in0, s1_, s2_, op0, op1=None, reads=(), writes=(), waits=()):
            if op1 is None:
                return op(eng, lambda e: e.tensor_scalar(out=out, in0=in0, scalar1=s1_, scalar2=None, op0=op0), reads, writes, waits)
            return op(eng, lambda e: e.tensor_scalar(out=out, in0=in0, scalar1=s1_, scalar2=s2_, op0=op0, op1=op1), reads, writes, waits)

        def stt(eng, out, in0, scalar, in1, op0, op1, reads=(), writes=(), waits=()):
            return op(eng, lambda e: e.scalar_tensor_tensor(out=out, in0=in0, scalar=scalar, in1=in1, op0=op0, op1=op1), reads, writes, waits)

        def cp(eng, out, in_, reads=(), writes=(), waits=()):
            if eng == "act":
                return act(out, in_, AF.Identity, reads, writes, waits)
            return op(eng, lambda e: e.tensor_copy(out=out, in_=in_), reads, writes, waits)

        def memset(eng, ap, val, writes=(), waits=()):
            return op(eng, lambda e: e.memset(ap, val), (), writes, waits)

        def mmf(out, lhsT, rhs, start=True, stop=True):
            return lambda e: e.matmul(out, lhsT=lhsT, rhs=rhs, start=start, stop=stop)

        def trf(out, in_):
            return lambda e: e.transpose(out=out, in_=in_, identity=identb.ap)

        def debug_dump(name, buf, ap, shape):
            if not dbg:
                return
            d = nc.dram_tensor("dbg_" + name, list(shape), F32, kind="ExternalOutput").ap()
            dbg_t.append(dma("pool", "d_dbg_" + name, d, ap, reads=list(buf)))

        gens = []

        def spawn(g):
            gens.append(g)

        def step_all():
            for g in list(gens):
                try:
                    next(g)
                except StopIteration:
                    gens.remove(g)

        def flush():
            n = 0
            while gens:
                step_all()
                n += 1
                assert n < 10000, "scheduler stuck"

        def take_now(reqs):
            n = 0
            while not all(len(p.free) >= k for p, k in reqs):
                step_all()
                n += 1
                assert n < 10000, "take_now stuck"
            out = []
            for p, k in reqs:
                for _ in range(k):
                    out.append(p.free.pop(0))
            return out

        def ensure(*keys):
            n = 0
            while not all(k in state for k in keys):
                step_all()
                n += 1
                assert n < 10000, ("ensure stuck", keys)

        def await_keys(*keys):
            while not all(k in state for k in keys):
                yield

        def wait_state(state, key):
            n = 0
            while key not in state:
                step_all()
                n += 1
                assert n < 10000, "wait_state stuck " + key
            return state.pop(key)

        hT = A.alloc([KC, 1024], BF16)
        yT = A.alloc([KC, 1024], BF16)
        hT_b = [Buf(hT[:, :, t * 128:(t + 1) * 128]) for t in range(NT)]
        yT_b = [[Buf(yT[:, fc, t * 128:(t + 1) * 128]) for t in range(NT)] for fc in range(KC)]
        WB = [Buf(A.alloc([KC, 512], BF16)) for _ in range(NWB)]
        xres = [Buf(A.alloc([D], F32)) for _ in range(2)]
        xnp = BankPool([Buf(A.alloc([D], BF16))])
        cst = Buf(A.alloc([384], F32))
        cst2 = Buf(A.alloc([16], F32))
        identb = Buf(A.alloc([128], BF16))
        trib = Buf(A.alloc([128], BF16))
        tripb = Buf(A.alloc([128], BF16))
        trip0b = Buf(A.alloc([128], BF16))
        smallc = Buf(A.alloc([8], F32))
        posi = Buf(A.alloc([16], I32))
        posf = Buf(A.alloc([16], F32))
        ang = Buf(A.alloc([16, 8], F32))
        ang2 = Buf(A.alloc([16, 8], F32))
        cosT = Buf(A.alloc([16, 8], F32))
        sinT = Buf(A.alloc([16, 8], F32))
        cTs = Buf(A.alloc([16], F32))
        scT = Buf(A.alloc([KC, 1], BF16))
        rowsb = Buf(A.alloc([512], F32, parts=1))
        onesr = Buf(A.alloc([128], F32, parts=1))
        GG1 = Buf(A.alloc([D], F32))
        GGs = [GG1, GG1]
        gpreT = Buf(A.alloc([2, KC], F32))
        GT = [Buf(A.alloc([KC], F32)) for _ in range(2)]
        shT = [Buf(A.alloc([KC], F32)) for _ in range(2)]
        sinks_sb = Buf(A.alloc([2, 16], F32))
        esink = Buf(A.alloc([2, 16], F32))
        sgWm = Buf(A.alloc([8, 128], BF16))
        sgb_sb = Buf(A.alloc([2, 8], F32))
        lng_sb = Buf(A.alloc([512], F32))
        lnb_sb = Buf(A.alloc([512], F32))
        wup_b = Buf(A.alloc([2, 256], BF16, parts=17))
        gng_sb = Buf(A.alloc([2, 128], F32))
        wing = Buf(A.alloc([2, KC, 16], BF16))
        cgT = Buf(A.alloc([1024], BF16, parts=32))
        Sst = Buf(A.alloc([4, 128], F32))
        ebl_all = Buf(A.alloc([NT, 4], F32))
        ssN = Buf(A.alloc([NT], F32))
        lnN = Buf(A.alloc([NT], F32))
        rstdN = Buf(A.alloc([NT], F32))
        oss = Buf(A.alloc([NT, 4], F32))
        ossum = Buf(A.alloc([NT], F32))
        rstdO = Buf(A.alloc([NT], F32))
        s1 = Buf(A.alloc([NT], F32))
        s2 = Buf(A.alloc([NT], F32))
        gstat = Buf(A.alloc([NT, 4], F32))
        ssg = Buf(A.alloc([NT, 4], F32))
        rstdg = Buf(A.alloc([NT, 4], F32))
        den = Buf(A.alloc([4, 8], F32))
        junk = Buf(A.alloc([512], BF16))
        halo = Buf(A.alloc([832], BF16))
        oc_one = Buf(A.alloc([4, 128], F32))
        ycp = BankPool([Buf(A.alloc([512], BF16)) for _ in range(2)])
        X_SIZE = 35840
        Y_SIZE = 20480
        X_BASE = A.off
        A.off += X_SIZE
        Y_BASE = A.off
        A.off += Y_SIZE
        assert A.off <= ARENA_BYTES, A.off
        xnp.put(Buf(A.at(Y_BASE + 12288, [D], BF16)))
        xnp.put(Buf(A.at(Y_BASE + 8192, [D], BF16)))

        def region_gla_t():
            s = Sub(A, X_BASE, X_SIZE)
            r = {}
            r["Vg"] = [Buf(s.alloc([512], BF16)) for _ in range(NT)]
            r["sp"] = [Buf(s.alloc([256], F32)) for _ in range(NT)]
            r["gqk"] = BankPool([Buf(s.alloc([512], F32)) for _ in range(2)])
            r["eb"] = Buf(s.alloc([256], F32))
            r["enb"] = Buf(s.alloc([256], F32))
            r["ec"] = Buf(s.alloc([256], F32))
            r["qkd"] = BankPool([Buf(s.alloc([3, 256], BF16)) for _ in range(3)])
            r["AT"] = BankPool([Buf(s.alloc([4, 128], BF16)) for _ in range(2)])
            r["keT"] = BankPool([Buf(s.alloc([4, 128], BF16)) for _ in range(2)])
            r["Sbfp"] = BankPool([Buf(s.alloc([4, 128], BF16)) for _ in range(3)])
            return r

        def region_swa():
            s = Sub(A, X_BASE, X_SIZE)
            r = {}
            kT = s.alloc([4, 1024], BF16)
            r["kT"] = kT
            r["kT_b"] = [Buf(kT[:, :, t * 128:(t + 1) * 128]) for t in range(NT)]
            r["Vb"] = [Buf(s.alloc([4, 80], BF16)) for _ in range(NT)]
            r["PT"] = BankPool([Buf(s.alloc([4, 128], BF16)) for _ in range(8)])
            r["qtmp"] = [Buf(s.alloc([8, 64], F32)) for _ in range(1)]
            r["qr"] = BankPool([Buf(s.alloc([8, 64], BF16)) for _ in range(2)])
            r["qT"] = BankPool([Buf(s.alloc([8, 128], BF16)) for _ in range(2)])
            r["ktmp"] = [Buf(s.alloc([4, 64], F32)) for _ in range(1)]
            r["kr"] = BankPool([Buf(s.alloc([4, 64], BF16)) for _ in range(2)])
            r["ya"] = BankPool([Buf(s.alloc([8, 64], BF16)) for _ in range(2)])
            r["ra"] = [Buf(s.alloc([8, 8], F32)) for _ in range(4)]
            return r


        def region_gla_p():
            s = Sub(A, Y_BASE, Y_SIZE)
            r = {"op": [Buf(s.alloc([4, 128], F32)) for _ in range(NT)]}
            r["qeTp"] = BankPool([Buf(s.alloc([4, 128], BF16)) for _ in range(3)])
            return r

        def region_gmlp():
            s = Sub(A, Y_BASE, Y_SIZE)
            r = {}
            r["gv"] = BankPool([Buf(s.alloc([512], F32)) for _ in range(2)])
            r["vn"] = [Buf(s.alloc([512], BF16)) for _ in range(NT)]
            r["gu"] = BankPool([Buf(s.alloc([512], BF16)) for _ in range(2)])
            r["yb"] = BankPool([Buf(s.alloc([512], BF16)) for _ in range(2)])
            r["tmx"] = [Buf(s.alloc([8, 64], F32)) for _ in range(1)]
            return r

        def region_fin():
            s = Sub(A, Y_BASE, Y_SIZE)
            return {"tmp": [Buf(s.alloc([D], F32)) for _ in range(1)]}

        def barrier(snap):
            for e in ("pe", "act", "dve", "pool", "sp"):
                P.wait_only(e, [snap, list(dbg_t)])

        stream = []
        for ci in range(8):
            stream.append((wmod_d[ci], 512))
        for l in range(L):
            for p in range(2):
                for j, name in enumerate(CHUNKS):
                    stream.append((win_d[l * 11 + j], 512))
                    if name == "kv" and p == 0:
                        for ci in range(8, 12):
                            stream.append((wmod_d[l * 12 + ci], 512))
                    if name == "kv" and p == 1 and l + 1 < L:
                        for ci in range(8):
                            stream.append((wmod_d[(l + 1) * 12 + ci], 512))
                for n in range(4):
                    stream.append((wout_d[l * 4 + n], 512))
        st_state = {"issued": 0, "consumed": 0}

        def wb_issue():
            k = st_state["issued"]
            if k >= len(stream):
                return
            src, ncols = stream[k]
            b = WB[k % NWB]
            dma("pool", f"d_wb{k % NWB}", b.ap[:, :, 0:ncols], src.rearrange("p (k n) -> p k n", k=KC), writes=[b])
            st_state["issued"] += 1

        def wb_consume():
            k = st_state["consumed"]
            assert k < st_state["issued"], (k, st_state)
            st_state["consumed"] += 1
            return WB[k % NWB]

        for _ in range(NWB):
            wb_issue()
        SX = Sub(A, X_BASE, X_SIZE)
        cstA = Buf(SX.alloc([256], F32))
        wup_f = Buf(SX.alloc([2, 256], F32, parts=17))
        dma("sp", "d_c0", cst.ap, cst_d[:, 256:640], writes=[cst])
        dma("sp", "d_c0a", cstA.ap, cst_d[:, 0:256], writes=[cstA])
        dma("sp", "d_c1", cst2.ap, cst2_d, writes=[cst2])
        dma("sp", "d_c2", posi.ap, pos_d, writes=[posi])
        dma("sp", "d_c3", cTs.ap, cT_d, writes=[cTs])
        dma("sp", "d_c7", gpreT.ap, gpreT_d.rearrange("p (a b) -> p a b", a=2), writes=[gpreT])
        dma("sp", "d_c8", sinks_sb.ap, sinks_d.rearrange("p (a b) -> p a b", a=2), writes=[sinks_sb])
        dma("sp", "d_c9", sgb_sb.ap, sgb_d.rearrange("p (a b) -> p a b", a=2), writes=[sgb_sb])
        dma("sp", "d_c12", wup_f.ap, wup_d.rearrange("p (a b) -> p a b", a=2), writes=[wup_f])
        dma("sp", "d_c13", gng_sb.ap, gng_d.rearrange("p (a b) -> p a b", a=2), writes=[gng_sb])
        dma("pool", "d_c14", wing.ap, wing_d.rearrange("l p (k n) -> p l k n", k=KC), writes=[wing])

        identf = cstA.ap[:, 0:128]
        tripf = cstA.ap[:, 128:256]
        trif = cst.ap[:, 0:128]
        ntri16 = cst.ap[:, 128:256]
        nup16 = cst.ap[:, 256:384]
        invf = cst2.ap[:, 8:16]
        cp("dve", identb.ap, identf, reads=[cstA], writes=[identb])
        cp("dve", trib.ap, trif, reads=[cst], writes=[trib])
        cp("dve", tripb.ap, tripf, reads=[cstA], writes=[tripb])
        memset("dve", trip0b.ap, 0.0, writes=[trip0b])
        memset("dve", onesr.ap, 1.0, writes=[onesr])
        memset("dve", smallc.ap[:, 0:1], -1.0 / 16.0, writes=[smallc])
        memset("dve", smallc.ap[:, 1:2], EPS, writes=[smallc])
        memset("dve", smallc.ap[:, 2:3], -PI, writes=[smallc])
        nsix = smallc.ap[:, 0:1]
        epsc = smallc.ap[:, 1:2]
        negpi = smallc.ap[:, 2:3]
        memset("dve", cgT.ap, 1.0, writes=[cgT])
        cp("dve", wup_b.ap, wup_f.ap, reads=[wup_f], writes=[wup_b])
        cp("dve", posf.ap, posi.ap, reads=[posi], writes=[posf])
        tt("dve", ang.ap, posf.ap.unsqueeze(2).to_broadcast([128, 16, 8]), invf.unsqueeze(1).to_broadcast([128, 16, 8]),
           ALU.mult, reads=[posf, cst2], writes=[ang])
        ts("dve", ang2.ap, ang.ap, 0.5 * PI, None, ALU.add, reads=[ang], writes=[ang2])
        MAGIC = 8388608.0
        for src, dst in ((ang, sinT), (ang2, cosT)):
            ts("dve", dst.ap, src.ap, 1.0 / (2 * PI), None, ALU.mult, reads=[src], writes=[dst])
            ts("dve", dst.ap, dst.ap, MAGIC, None, ALU.add, writes=[dst])
            ts("dve", dst.ap, dst.ap, -MAGIC, None, ALU.add, writes=[dst])
            stt("dve", src.ap, dst.ap, -2 * PI, src.ap, ALU.mult, ALU.add, reads=[dst], writes=[src])
            ts("dve", src.ap, src.ap, -PI, PI, ALU.max, ALU.min, writes=[src])
            act(dst.ap, src.ap, AF.Sin, reads=[src], writes=[dst])
        act(esink.ap, sinks_sb.ap, AF.Exp, reads=[sinks_sb], writes=[esink])
        act(scT.ap, cTs.ap.rearrange("p (k b) -> p k b", k=KC), AF.Silu, reads=[cTs], writes=[scT])

        modT_bank = {}

        def mod_compute(l, part):
            chunks = range(8) if part == "ss" else range(8, 12)
            if part == "ss":
                (mt,) = take_now([(mixp, 1)])
            else:
                dma("sp", "d_gp", GGs[l].ap, gpost_d[:, l * D:(l + 1) * D], writes=[GGs[l]])
            for ci in chunks:
                wbb = wb_consume()
                (mb,) = take_now([(mixp, 1)])
                dma("sp", "d_bm", rowsb.ap, bmod_d[l][:, ci * 512:(ci + 1) * 512], writes=[rowsb])
                items = [mmf(mb.ap[0:1, :], scT.ap[:, kc, :], wbb.ap[:, kc, :], kc == 0, kc == KC - 1) for kc in range(KC)]
                pe_group(items, reads=[scT, wbb], writes=[mb])
                wb_issue()
                tt("dve", rowsb.ap, mb.ap[0:1, :], rowsb.ap, ALU.add, reads=[mb], writes=[rowsb])
                mixp.put(mb)
                if part == "ss":
                    items = [mmf(mt.ap[:, ci * 4 + j: ci * 4 + j + 1], rowsb.ap[0:1, j * 128:(j + 1) * 128], onesr.ap[0:1, 0:1]) for j in range(4)]
                    pe_group(items, reads=[rowsb, onesr], writes=[mt])
                else:
                    (gb,) = take_now([(mixp, 1)])
                    pe_group([mmf(gb.ap, onesr.ap[0:1, :], rowsb.ap[0:1, :])], reads=[rowsb, onesr], writes=[gb])
                    c0 = (ci - 8) * 512
                    tt("dve", GGs[l].ap[:, c0:c0 + 512], gb.ap, GGs[l].ap[:, c0:c0 + 512], ALU.mult, reads=[gb], writes=[GGs[l]])
                    mixp.put(gb)
                step_all()
            if part == "ss":
                cp("dve", shT[l].ap, mt.ap[:, 0:16], reads=[mt], writes=[shT[l]])
                stt("dve", GT[l].ap, mt.ap[:, 16:32], 1.0, gpreT.ap[:, l, :], ALU.add, ALU.mult,
                    reads=[mt, gpreT], writes=[GT[l]])
                mixp.put(mt)

        import os as _os
        PN_LEVEL = int(_os.environ.get("PN_LEVEL", "9"))

        def phase_n(l, t, xb, xn, pid):
            if PN_LEVEL == 0:
                xnp.put(xn)
                state["hT", pid, t] = True
                yield
                return
            if PN_LEVEL != 15:
                memset("dve", ssN.ap[:, t:t + 1], 0.0, writes=[ssN])
            if PN_LEVEL == 14:
                pass
            elif PN_LEVEL == 12:
                act(xn.ap, xb.ap, AF.Identity, reads=[xb], writes=[xn, ssN])
            elif PN_LEVEL in (13, 15):
                cp("dve", xn.ap, xb.ap, reads=[xb], writes=[xn])
            else:
                act(xn.ap, xb.ap, AF.Square, reads=[xb], writes=[xn, ssN], accum_out=ssN.ap[:, t:t + 1])
            if PN_LEVEL in (10, 12, 13, 14, 15):
                xnp.put(xn)
                state["hT", pid, t] = True
                yield
                return
            act(lnN.ap[:, t:t + 1], ssN.ap[:, t:t + 1], AF.Ln, reads=[ssN, smallc], writes=[lnN], scale=1.0 / D, bias=epsc)
            act(rstdN.ap[:, t:t + 1], lnN.ap[:, t:t + 1], AF.Exp, reads=[lnN], writes=[rstdN], scale=-0.5)
            if PN_LEVEL == 11:
                xnp.put(xn)
                state["hT", pid, t] = True
                yield
                return
            ts("dve", xn.ap, xb.ap, rstdN.ap[:, t:t + 1], None, ALU.mult, reads=[xb, rstdN], writes=[xn])
            yield
            if PN_LEVEL == 1:
                xnp.put(xn)
                state["hT", pid, t] = True
                yield
                return
            for u in range(4):
                (tb,) = yield from take([(tbp, 1)])
                items = [trf(tb.bf[:, i * 128:(i + 1) * 128], xn.ap[:, (u * 4 + i) * 128:(u * 4 + i + 1) * 128]) for i in range(4)]
                pe_group(items, reads=[xn, identb], writes=[tb])
                if u == 3:
                    xnp.put(xn)
                yield
                for i in range(4 if PN_LEVEL >= 3 else 0):
                    kc = u * 4 + i
                    if u % 2 == 0:
                        act(hT[:, kc, t * 128:(t + 1) * 128], tb.bf[:, i * 128:(i + 1) * 128], AF.Identity,
                            reads=[tb, GT[l], shT[l]], writes=[hT_b[t]],
                            scale=GT[l].ap[:, kc:kc + 1], bias=shT[l].ap[:, kc:kc + 1])
                    else:
                        ts("dve", hT[:, kc, t * 128:(t + 1) * 128], tb.bf[:, i * 128:(i + 1) * 128],
                           GT[l].ap[:, kc:kc + 1], shT[l].ap[:, kc:kc + 1], ALU.mult, ALU.add,
                           reads=[tb, GT[l], shT[l]], writes=[hT_b[t]])
                tbp.put(tb)
            state["hT", pid, t] = True
            if dbg and l == 0 and t == NT - 1:
                debug_dump("hT", hT_b, hT, [128, KC, 1024])

        def tok_group(wbb, t, pid):
            ensure(("hT", pid, t))
            acc = next_acc()
            items = [mmf(acc.ap, hT[:, kc, t * 128:(t + 1) * 128], wbb.ap[:, kc, :], kc == 0, kc == KC - 1) for kc in range(KC)]
            pe_group(items, reads=[hT_b[t], wbb], writes=[acc])
            return acc

        def feat_group(wbuf, lhs, hf, pid, ncol=128):
            ensure(*[("hT", pid, 4 * hf + i) for i in range(4)])
            acc = next_acc()
            items = [mmf(acc.ap[0:ncol, :], lhs(kc), hT[:, kc, hf * 512:(hf + 1) * 512], kc == 0, kc == KC - 1) for kc in range(KC)]
            pe_group(items, reads=[hT_b[4 * hf + i] for i in range(4)] + [wbuf], writes=[acc])
            return acc

        x1_written = [None] * (2 * NT)
        state = {}
        rr_den = [0]

        def mark_done(key, n, snapname):
            state[key] = state.get(key, 0) + 1
            if state[key] == n:
                del state[key]
                state[snapname] = P.snapshot()

        def run_pass(l, p):
            pid = 2 * l + p
            src_d = x_d if l == 0 else x1_d
            dst_d = x1_d if l + 1 < L else y_d
            if pid > 0:
                barrier(wait_state(state, "snapXY"))
            else:
                barrier(state.pop("setup"))
            RT = region_gla_t()
            RP = region_gla_p()
            if p == 0:
                memset("dve", Sst.ap, 0.0, writes=[Sst])
            sb0 = RT["Sbfp"].free.pop(0)
            cp("dve", sb0.ap[0:64], Sst.ap[0:64], reads=[Sst], writes=[sb0])
            state["Sbf", pid, 0] = sb0

            def pn_all():
                def load(t):
                    b = xres[t % 2]
                    w = [x1_written[p * NT + t]] if l > 0 else []
                    dma("sp", f"d_xr{t % 2}", b.ap, src_d[p * 1024 + t * 128: p * 1024 + (t + 1) * 128, :], writes=[b], waits=w)
                load(0)
                load(1)
                for t in range(NT):
                    (xn,) = yield from take([(xnp, 1)])
                    pn = phase_n(l, t, xres[t % 2], xn, pid)
                    next(pn)
                    if t + 2 < NT:
                        load(t + 2)
                    spawn(pn)
                    yield

            spawn(pn_all())
            for _ in range(6):
                step_all()
            for b_ in (oss, s1, s2, ssg):
                memset("dve", b_.ap, 0.0, writes=[b_])
            dma("sp", "d_lng", lng_sb.ap, lng_d[:, l * 512:(l + 1) * 512], writes=[lng_sb])
            dma("sp", "d_lnb", lnb_sb.ap, lnb_d[:, l * 512:(l + 1) * 512], writes=[lnb_sb])

            for hf in range(2):
                acc = feat_group(wing, lambda kc: wing.ap[:, l, kc, :], hf, pid, ncol=16)
                cp("act", cgT.ap[0:16, hf * 512:(hf + 1) * 512], acc.ap[0:16, :], reads=[acc], writes=[cgT])
                step_all()

            def sp_chain(t):
                (mb,) = yield from take([(mixp, 1)])
                pe_group([mmf(mb.ap[:, 0:256], cgT.ap[0:17, t * 128:(t + 1) * 128], wup_b.ap[0:17, l, :])],
                         reads=[cgT, wup_b], writes=[mb])
                yield
                spb = RT["sp"][t]
                act(spb.ap, mb.ap[:, 0:256], AF.Exp, reads=[mb], writes=[spb], scale=-1.0)
                mixp.put(mb)
                act(spb.ap, spb.ap, AF.Ln, writes=[spb], bias=1.0, scale=1.0)
                state["sp", pid, t] = True

            def gla_chain(t, acc, gq):
                cp("act", gq.ap, acc.ap, reads=[acc], writes=[gq])
                if l == 0 and t == 1:
                    debug_dump("gq", [gq], gq.ap, [128, 512])
                yield
                yield from await_keys(("sp", pid, t))
                spb = RT["sp"][t]
                bc, bl, qkd = yield from take([(mixp, 2), (RT["qkd"], 1)])
                pe_group([mmf(bc.ap[:, 0:256], ntri16, spb.ap), mmf(bc.ap[:, 256:512], nup16, spb.ap)],
                         reads=[cst, spb], writes=[bc])
                pe_group([mmf(bl.ap[0:64, h:h + 1], spb.ap[:, h * 64:(h + 1) * 64], nsix) for h in range(4)],
                         reads=[spb, smallc], writes=[bl])
                yield
                eb, enb, ec = RT["eb"], RT["enb"], RT["ec"]
                act(eb.ap, bc.ap[:, 0:256], AF.Exp, reads=[bc], writes=[eb])
                act(enb.ap, bc.ap[:, 0:256], AF.Exp, reads=[bc], writes=[enb], scale=-1.0)
                act(ec.ap, bc.ap[:, 256:512], AF.Exp, reads=[bc], writes=[ec])
                act(ebl_all.ap[0:64, t, :], bl.ap[0:64, 0:4], AF.Exp, reads=[bl], writes=[ebl_all])
                mixp.put(bc, bl)
                qe, ke, kd = qkd.ap[:, 0, :], qkd.ap[:, 1, :], qkd.ap[:, 2, :]
                stt("dve", qe, gq.ap[:, 0:256], 0.125, eb.ap, ALU.mult, ALU.mult, reads=[gq, eb], writes=[qkd])
                tt("dve", ke, gq.ap[:, 256:512], enb.ap, ALU.mult, reads=[gq, enb], writes=[qkd])
                tt("dve", kd, gq.ap[:, 256:512], ec.ap, ALU.mult, reads=[gq, ec], writes=[qkd])
                RT["gqk"].put(gq)
                yield
                reqs = [(mixp, 3), (RT["keT"], 1), (RP["qeTp"], 1)] + ([(RT["Sbfp"], 1)] if t + 1 < NT else [])
                got = yield from take(reqs)
                tbq, tbk, ds, keT, qeT = got[0:5]
                sb_next = got[5] if t + 1 < NT else None
                pe_group([trf(tbq.bf[0:64, h * 128:(h + 1) * 128], qe[:, h * 64:(h + 1) * 64]) for h in range(4)],
                         reads=[qkd, identb], writes=[tbq])
                pe_group([trf(tbk.bf[0:64, h * 128:(h + 1) * 128], ke[:, h * 64:(h + 1) * 64]) for h in range(4)],
                         reads=[qkd, identb], writes=[tbk])
                Vg = RT["Vg"][t]
                pe_group([mmf(ds.ap[0:64, h * 128:(h + 1) * 128], kd[:, h * 64:(h + 1) * 64], Vg.ap[:, h * 128:(h + 1) * 128]) for h in range(4)],
                         reads=[qkd, Vg], writes=[ds])
                RT["qkd"].put(qkd)
                yield
                if t > 0:
                    yield from await_keys(("Sdone", pid, t - 1))
                cp("dve", qeT.ap[0:64], tbq.bf[0:64, :].rearrange("p (a b) -> p a b", a=4), reads=[tbq], writes=[qeT])
                cp("dve", keT.ap[0:64], tbk.bf[0:64, :].rearrange("p (a b) -> p a b", a=4), reads=[tbk], writes=[keT])
                mixp.put(tbq, tbk)
                for h in range(4):
                    stt("dve", Sst.ap[0:64, h, :], Sst.ap[0:64, h, :], ebl_all.ap[0:64, t, h:h + 1], ds.ap[0:64, h * 128:(h + 1) * 128],
                        ALU.mult, ALU.add, reads=[ebl_all, ds], writes=[Sst])
                mixp.put(ds)
                if t + 1 < NT:
                    cp("dve", sb_next.ap[0:64], Sst.ap[0:64], reads=[Sst], writes=[sb_next])
                    state["Sbf", pid, t + 1] = sb_next
                state["Sdone", pid, t] = True
                yield
                sc, AT = yield from take([(mixp, 1), (RT["AT"], 1)])
                pe_group([mmf(sc.ap[:, h * 128:(h + 1) * 128], keT.ap[0:64, h, :], qeT.ap[0:64, h, :]) for h in range(4)],
                         reads=[keT, qeT], writes=[sc])
                RT["keT"].put(keT)
                yield
                tt("dve", AT.ap, sc.ap.rearrange("p (a b) -> p a b", a=4), trib.ap.unsqueeze(1).to_broadcast([128, 4, 128]),
                   ALU.mult, reads=[sc, trib], writes=[AT])
                mixp.put(sc)
                yield
                (ob,) = yield from take([(mixp, 1)])
                Sb = state["Sbf", pid, t]
                items = []
                for h in range(4):
                    items.append(mmf(ob.ap[:, h * 128:(h + 1) * 128], AT.ap[:, h, :], Vg.ap[:, h * 128:(h + 1) * 128], True, False))
                    items.append(mmf(ob.ap[:, h * 128:(h + 1) * 128], qeT.ap[0:64, h, :], Sb.ap[0:64, h, :], False, True))
                pe_group(items, reads=[AT, Vg, qeT, Sb], writes=[ob])
                RT["AT"].put(AT)
                RP["qeTp"].put(qeT)
                RT["Sbfp"].put(Sb)
                yield
                cp("act", RP["op"][t].ap, ob.ap.rearrange("p (a b) -> p a b", a=4), reads=[ob], writes=[RP["op"][t]])
                mixp.put(ob)
                state["op", pid, t] = True
                mark_done("gla_done", NT, "snapX_glat")

            def corr_chain(t):
                yield from await_keys(("op", pid, t))
                (yc,) = yield from take([(ycp, 1)])
                oc = oc_one
                op_b = RP["op"][t]
                for h in range(4):
                    act(junk.ap[:, 0:128], op_b.ap[:, h, :], AF.Square, reads=[op_b], writes=[junk, ssg], accum_out=ssg.ap[:, t, h:h + 1])
                act(rstdg.ap[:, t, :], ssg.ap[:, t, :], AF.Ln, reads=[ssg, smallc], writes=[rstdg], scale=1.0 / 128, bias=epsc)
                act(rstdg.ap[:, t, :], rstdg.ap[:, t, :], AF.Exp, writes=[rstdg], scale=-0.5)
                tt("dve", oc.ap, op_b.ap, rstdg.ap[:, t, :].unsqueeze(2).to_broadcast([128, 4, 128]), ALU.mult,
                   reads=[op_b, rstdg], writes=[oc])
                tt("dve", yc.ap.rearrange("p (a b) -> p a b", a=4), oc.ap,
                   gng_sb.ap[:, l, :].unsqueeze(1).to_broadcast([128, 4, 128]), ALU.mult, reads=[oc, gng_sb], writes=[yc])
                yield
                (tb,) = yield from take([(tbp, 1)])
                pe_group([trf(tb.bf[:, i * 128:(i + 1) * 128], yc.ap[:, i * 128:(i + 1) * 128]) for i in range(4)],
                         reads=[yc, identb], writes=[tb])
                ycp.put(yc)
                yield
                yv = yT[:, 12:16, t * 128:(t + 1) * 128]
                tt("dve", yv, yv, tb.bf.rearrange("p (a b) -> p a b", a=4), ALU.mult,
                   reads=[tb], writes=[yT_b[12 + i][t] for i in range(4)])
                tbp.put(tb)
                mark_done("corr_done", NT, "snapY_glap")

            def z_chunk(name):
                wbb = wb_consume()
                fc0 = Z_FC0[name]
                for m in range(4):
                    for hf in range(2):
                        acc = feat_group(wbb, lambda kc, m=m: wbb.ap[:, kc, m * 128:(m + 1) * 128], hf, pid)
                        if m == 3 and hf == 1:
                            wb_issue()
                        act(yT[:, fc0 + m, hf * 512:(hf + 1) * 512], acc.ap, AF.Silu, reads=[acc],
                            writes=[yT_b[fc0 + m][4 * hf + i] for i in range(4)])
                        step_all()
                if stop == name:
                    raise StopBuild()

            def rope4(src, nh, t, ra):
                C = cosT.ap[:, p * NT + t, :].unsqueeze(1).to_broadcast([128, nh, 8])
                S = sinT.ap[:, p * NT + t, :].unsqueeze(1).to_broadcast([128, nh, 8])
                k1, k2 = src.ap[:, :, 0:8], src.ap[:, :, 8:16]
                vs_ = [r_.ap[:, 0:nh, :] for r_ in ra]
                tt("dve", vs_[0], k1, C, ALU.mult, reads=[src, cosT], writes=[ra[0]])
                tt("dve", vs_[1], k2, S, ALU.mult, reads=[src, sinT], writes=[ra[1]])
                tt("dve", vs_[2], k2, C, ALU.mult, reads=[src, cosT], writes=[ra[2]])
                tt("dve", vs_[3], k1, S, ALU.mult, reads=[src, sinT], writes=[ra[3]])
                return vs_

            def kv_chain(t, acc, RS, kr):
                Vb = RS["Vb"][t]
                cp("act", Vb.ap[:, :, 0:64], acc.ap[:, 256:512].rearrange("p (a b) -> p a b", a=4), reads=[acc], writes=[Vb])
                ktmp = RS["ktmp"][0]
                cp("act", ktmp.ap, acc.ap[:, 0:256].rearrange("p (a b) -> p a b", a=4), reads=[acc], writes=[ktmp])
                KV_LEVEL = int(_os.environ.get("KV_LEVEL", "9"))
                if KV_LEVEL == 0:
                    RS["kr"].put(kr)
                    state["kT", pid, t] = True
                    yield
                    return
                ra = RS["ra"]
                av, bv, cv_, dv = rope4(ktmp, 4, t, ra)
                tt("dve", kr.ap[:, :, 0:8], av, bv, ALU.subtract, reads=[ra[0], ra[1]], writes=[kr])
                tt("dve", kr.ap[:, :, 8:16], cv_, dv, ALU.add, reads=[ra[2], ra[3]], writes=[kr])
                cp("dve", kr.ap[:, :, 16:64], ktmp.ap[:, :, 16:64], reads=[ktmp], writes=[kr])
                yield
                if KV_LEVEL == 1:
                    RS["kr"].put(kr)
                    state["kT", pid, t] = True
                    return
                (tb,) = yield from take([(tbp, 1)])
                pe_group([trf(tb.bf[0:64, g * 128:(g + 1) * 128], kr.ap[:, g, :]) for g in range(4)],
                         reads=[kr, identb], writes=[tb])
                RS["kr"].put(kr)
                if l == 0 and t == 1:
                    debug_dump("kr", [kr], kr.ap, [128, 4, 2, 64])
                    debug_dump("Vb", [Vb], Vb.ap, [128, 4, 65])
                yield
                cp("act", RS["kT_b"][t].ap[0:64], tb.bf[0:64, :].rearrange("p (a b) -> p a b", a=4), reads=[tb], writes=[RS["kT_b"][t]])
                tbp.put(tb)
                state["kT", pid, t] = True
                if t == NT - 1 and p == 0 and KV_LEVEL >= 3:
                    cp("dve", halo.ap[0:64, 0:512].rearrange("p (a b) -> p a b", a=4), RS["kT_b"][t].ap[0:64], reads=[RS["kT_b"][t]], writes=[halo])
                    cp("dve", halo.ap[:, 512:832].rearrange("p (a b) -> p a b", a=4)[:, :, 0:65], Vb.ap[:, :, 0:65], reads=[Vb], writes=[halo])
                    state["halo", pid + 1] = True

            def q_chain(c, t, acc, RS, qr):
                qtmp = RS["qtmp"][0]
                cp("act", qtmp.ap, acc.ap.rearrange("p (a b) -> p a b", a=8), reads=[acc], writes=[qtmp])
                ra = RS["ra"]
                av, bv, cv_, dv = rope4(qtmp, 8, t, ra)
                tt("dve", qr.ap[:, :, 0:8], av, bv, ALU.subtract, reads=[ra[0], ra[1]], writes=[qr])
                tt("dve", qr.ap[:, :, 8:16], cv_, dv, ALU.add, reads=[ra[2], ra[3]], writes=[qr])
                cp("dve", qr.ap[:, :, 16:64], qtmp.ap[:, :, 16:64], reads=[qtmp], writes=[qr])
                yield
                tb, qT = yield from take([(tbp, 1), (RS["qT"], 1)])
                pe_group([trf(tb.bff[0:64, i * 128:(i + 1) * 128], qr.ap[:, i, :]) for i in range(8)],
                         reads=[qr, identb], writes=[tb])
                RS["qr"].put(qr)
                if l == 0 and t == 1 and c == 0:
                    debug_dump("qr", [qr], qr.ap, [128, 8, 64])
                yield
                cp("dve", qT.ap[0:64], tb.bff[0:64, :].rearrange("p (a b) -> p a b", a=8), reads=[tb], writes=[qT])
                tbp.put(tb)
                yield
                ya = None
                has_prev = not (p == 0 and t == 0)
                if has_prev:
                    yield from await_keys(("kT", pid, t), ("halo", pid) if t == 0 else ("kT", pid, t - 1))
                else:
                    yield from await_keys(("kT", pid, t))
                kbs = [0, 1] if has_prev else [1]
                for gl in range(2):
                    g = 2 * c + gl
                    sc0, sc1, pt0, pt1 = yield from take([(mixp, 2), (RS["PT"], 2)])
                    scs, pts = [sc0, sc1], [pt0, pt1]
                    for kb in kbs:
                        sc = scs[kb]
                        if kb == 0 and t == 0:
                            kTv = halo.ap[:, 0:512].rearrange("p (a b) -> p a b", a=4)[:, g, :]
                            kdep = [halo]
                        else:
                            tk = t - 1 + kb
                            kTv = RS["kT"][:, g, tk * 128:(tk + 1) * 128]
                            kdep = [RS["kT_b"][tk]]
                        items = [mmf(sc.ap.rearrange("p (a b) -> p a b", a=4), kTv[0:64, :], qT.ap[0:64, 4 * gl:4 * gl + 4, :])]
                        pe_group(items, reads=kdep + [qT], writes=[sc])
                    if gl == 1:
                        RS["qT"].put(qT)
                    yield
                    for kb in kbs:
                        PTb = pts[kb]
                        act(PTb.ap, scs[kb].ap.rearrange("p (a b) -> p a b", a=4), AF.Exp, reads=[scs[kb]], writes=[PTb], scale=0.125)
                        mask = tripb if kb == 0 else trib
                        tt("dve", PTb.ap, PTb.ap, mask.ap.unsqueeze(1).to_broadcast([128, 4, 128]), ALU.mult, reads=[mask], writes=[PTb])
                    mixp.put(*scs)
                    yield
                    if gl == 0:
                        pvb, ya = yield from take([(mixp, 1), (RS["ya"], 1)])
                    else:
                        (pvb,) = yield from take([(mixp, 1)])
                    pv = pvb.ap[:, 0:260].rearrange("p (a b) -> p a b", a=4)
                    items = []
                    for hh in range(4):
                        if t == 0:
                            Vp = halo.ap[:, 512:832].rearrange("p (a b) -> p a b", a=4)[:, g, 0:65]
                        elif has_prev:
                            Vp = RS["Vb"][t - 1].ap[:, g, 0:65]
                        Vc = RS["Vb"][t].ap[:, g, 0:65]
                        if has_prev:
                            items.append(mmf(pv[:, hh, :], pts[0].ap[:, hh, :], Vp, True, False))
                            items.append(mmf(pv[:, hh, :], pts[1].ap[:, hh, :], Vc, False, True))
                        else:
                            items.append(mmf(pv[:, hh, :], pts[1].ap[:, hh, :], Vc, True, True))
                    rd = [pts[1], RS["Vb"][t]]
                    if has_prev:
                        rd += [pts[0]] + ([halo] if t == 0 else [RS["Vb"][t - 1]])
                    pe_group(items, reads=rd, writes=[pvb])
                    RS["PT"].put(*pts)
                    yield
                    dslot = rr_den[0] % 4
                    rr_den[0] += 1
                    dn = den.ap[:, dslot, 0:4]
                    rdn = den.ap[:, dslot, 4:8]
                    tt("dve", dn, pv[:, :, 64], esink.ap[:, l, 4 * g:4 * g + 4], ALU.add, reads=[pvb, esink], writes=[den])
                    op("dve", lambda e, rdn=rdn, dn=dn: e.reciprocal(out=rdn, in_=dn), writes=[den])
                    tt("dve", ya.ap[:, 4 * gl:4 * gl + 4, :], pv[:, :, 0:64], rdn.unsqueeze(2).to_broadcast([128, 4, 64]),
                       ALU.mult, reads=[pvb, den], writes=[ya])
                    mixp.put(pvb)
                    yield
                (tb,) = yield from take([(tbp, 1)])
                pe_group([trf(tb.bf[:, i * 128:(i + 1) * 128], ya.ap[:, 2 * i:2 * i + 2, :].rearrange("p a b -> p (a b)")) for i in range(4)],
                         reads=[ya, identb], writes=[tb])
                RS["ya"].put(ya)
                if l == 0 and t == 1 and c == 0:
                    debug_dump("ya", [ya], ya.ap, [128, 8, 64])
                yield
                yv = yT[:, 4 * c:4 * c + 4, t * 128:(t + 1) * 128]
                tt("dve", yv, yv, tb.bf.rearrange("p (a b) -> p a b", a=4), ALU.mult,
                   reads=[tb], writes=[yT_b[4 * c + i][t] for i in range(4)])
                tbp.put(tb)
                mark_done("q_done", 2 * NT, "snapX_swa")

            def vs_chain(t, acc, RG, gv):
                act(gv.ap, acc.ap, AF.Gelu_apprx_tanh, reads=[acc], writes=[gv, s1], accum_out=s1.ap[:, t:t + 1])
                yield
                act(junk.ap, gv.ap, AF.Square, reads=[gv], writes=[junk, s2], accum_out=s2.ap[:, t:t + 1])
                mean, msq, var, rs = [gstat.ap[:, t, i:i + 1] for i in range(4)]
                ts("dve", mean, s1.ap[:, t:t + 1], 1.0 / 512, None, ALU.mult, reads=[s1], writes=[gstat])
                tt("dve", msq, mean, mean, ALU.mult, writes=[gstat])
                stt("dve", var, s2.ap[:, t:t + 1], 1.0 / 512, msq, ALU.mult, ALU.subtract, reads=[s2], writes=[gstat])
                yield
                act(var, var, AF.Ln, reads=[smallc], writes=[gstat], bias=epsc, scale=1.0)
                act(rs, var, AF.Exp, writes=[gstat], scale=-0.5)
                ts("dve", gv.ap, gv.ap, mean, rs, ALU.subtract, ALU.mult, reads=[gstat], writes=[gv])
                tt("dve", gv.ap, gv.ap, lng_sb.ap, ALU.mult, reads=[lng_sb], writes=[gv])
                tt("dve", RG["vn"][t].ap, gv.ap, lnb_sb.ap, ALU.add, reads=[gv, lnb_sb], writes=[RG["vn"][t]])
                if l == 0 and t == 1:
                    debug_dump("vn", [RG["vn"][t]], RG["vn"][t].ap, [128, 512])
                RG["gv"].put(gv)
                state["vn", pid, t] = True

            def u_chain(t, acc, RG, gu):
                act(gu.ap, acc.ap, AF.Gelu_apprx_tanh, reads=[acc], writes=[gu])
                yield
                yield from await_keys(("vn", pid, t))
                mx, yb = yield from take([(mixp, 1), (RG["yb"], 1)])
                vn = RG["vn"][t]
                pe_group([mmf(mx.ap[:, g * 64:(g + 1) * 64], sgWm.ap[:, g, :], vn.ap[:, g * 64:(g + 1) * 64]) for g in range(8)],
                         reads=[sgWm, vn], writes=[mx])
                yield
                tmx = RG["tmx"][0]
                tt("dve", tmx.ap, mx.ap.rearrange("p (a b) -> p a b", a=8), sgb_sb.ap[:, l, :].unsqueeze(2).to_broadcast([128, 8, 64]),
                   ALU.add, reads=[mx, sgb_sb], writes=[tmx])
                mixp.put(mx)
                tt("dve", yb.ap, tmx.ap.rearrange("p a b -> p (a b)"), gu.ap, ALU.mult, reads=[tmx, gu], writes=[yb])
                RG["gu"].put(gu)
                if l == 0 and t == 1:
                    debug_dump("yb", [yb], yb.ap, [128, 512])
                yield
                (tb,) = yield from take([(tbp, 1)])
                pe_group([trf(tb.bf[:, i * 128:(i + 1) * 128], yb.ap[:, i * 128:(i + 1) * 128]) for i in range(4)],
                         reads=[yb, identb], writes=[tb])
                RG["yb"].put(yb)
                yield
                yv = yT[:, 8:12, t * 128:(t + 1) * 128]
                tt("dve", yv, yv, tb.bf.rearrange("p (a b) -> p a b", a=4), ALU.mult,
                   reads=[tb], writes=[yT_b[8 + i][t] for i in range(4)])
                tbp.put(tb)
                state["ufin", pid, t] = True
                mark_done("u_done", NT, "snapY_gmlp")

            RS = None
            RG = None
            for name in CHUNKS:
                if name in Z_FC0:
                    z_chunk(name)
                    continue
                if name == "kv":
                    barrier(wait_state(state, "snapX_glat"))
                    RS = region_swa()
                    for t in range(NT):
                        memset("dve", RS["Vb"][t].ap, 1.0, writes=[RS["Vb"][t]])
                    for h2 in range(0 if "setup" in _os.environ.get("KV_SKIP", "") else 2):
                        bb = RS["qtmp"][0]
                        v4 = bb.ap.rearrange("p a b -> p (a b)").rearrange("p (a b) -> p a b", a=4)
                        dma("sp", "d_sgw", v4, sgw_d[l].rearrange("p (a b) -> p a b", a=8)[:, h2 * 4:(h2 + 1) * 4, :], writes=[bb])
                        tt("dve", sgWm.ap[:, h2 * 4:(h2 + 1) * 4, :], v4, trif.unsqueeze(1).to_broadcast([128, 4, 128]),
                           ALU.mult, reads=[bb, cst], writes=[sgWm])
                if name == "vs":
                    barrier(wait_state(state, "snapY_glap"))
                    RG = region_gmlp()
                wbb = wb_consume()
                for t in range(NT):
                    res = None
                    if name == "cqk":
                        (res,) = take_now([(RT["gqk"], 1)])
                    elif name == "kv":
                        (res,) = take_now([(RS["kr"], 1)])
                    elif name in ("qA", "qB"):
                        (res,) = take_now([(RS["qr"], 1)])
                    elif name == "vs":
                        (res,) = take_now([(RG["gv"], 1)])
                    elif name == "u":
                        (res,) = take_now([(RG["gu"], 1)])
                    acc = tok_group(wbb, t, pid)
                    if t == NT - 1:
                        wb_issue()
                    if name == "cv":
                        cp("act", RT["Vg"][t].ap, acc.ap, reads=[acc], writes=[RT["Vg"][t]])
                        spawn(sp_chain(t))
                    else:
                        if name == "cqk":
                            g = gla_chain(t, acc, res)
                        elif name == "kv":
                            g = kv_chain(t, acc, RS, res)
                            if "corr" not in _os.environ.get("KV_SKIP", ""):
                                spawn(corr_chain(t))
                        elif name in ("qA", "qB"):
                            g = q_chain(0 if name == "qA" else 1, t, acc, RS, res)
                        elif name == "vs":
                            g = vs_chain(t, acc, RG, res)
                        else:
                            g = u_chain(t, acc, RG, res)
                        next(g)
                        spawn(g)
                    step_all()
                if name == "kv" and p == 0 and "modg" not in _os.environ.get("KV_SKIP", ""):
                    mod_compute(l, "g")
                if name == "kv" and p == 1 and l + 1 < L:
                    mod_compute(l + 1, "ss")
                if stop == name:
                    raise StopBuild()

            wait_state(state, "snapX_swa")
            if dbg and l == 0:
                ensure(*[("ufin", pid, t) for t in range(NT)])
                d_ = nc.dram_tensor("dbg_yT", [128, KC, 1024], F32, kind="ExternalOutput").ap()
                dbg_t.append(dma("pool", "d_dbg_yT", d_, yT, reads=[yT_b[fc][t] for fc in range(KC) for t in range(NT)]))

            def xres_load(t):
                b = xres[t % 2]
                w = [x1_written[p * NT + t]] if l > 0 else []
                dma("sp", f"d_xr{t % 2}", b.ap, src_d[p * 1024 + t * 128: p * 1024 + (t + 1) * 128, :], writes=[b], waits=w)
                state["xres", pid, t] = True

            def fin_chain(t, RF):
                xb = xres[t % 2]
                yield from await_keys(("xres", pid, t))
                op("dve", lambda e: e.reduce_sum(out=ossum.ap[:, t:t + 1], in_=oss.ap[:, t, :], axis=AX.X), reads=[oss], writes=[ossum])
                act(rstdO.ap[:, t:t + 1], ossum.ap[:, t:t + 1], AF.Ln, reads=[ossum, smallc], writes=[rstdO], scale=1.0 / D, bias=epsc)
                act(rstdO.ap[:, t:t + 1], rstdO.ap[:, t:t + 1], AF.Exp, writes=[rstdO], scale=-0.5)
                tmp = RF["tmp"][0]
                stt("dve", tmp.ap.rearrange("p (a b) -> p a b", a=KC), hT[:, :, t * 128:(t + 1) * 128], rstdO.ap[:, t:t + 1],
                    GGs[l].ap.rearrange("p (a b) -> p a b", a=KC), ALU.mult, ALU.mult,
                    reads=[hT_b[t], rstdO, GGs[l]], writes=[tmp])
                tt("dve", xb.ap, tmp.ap, xb.ap, ALU.add, reads=[tmp], writes=[xb])
                x1_written[p * NT + t] = dma("sp", f"d_xo{t % 2}", dst_d[p * 1024 + t * 128: p * 1024 + (t + 1) * 128, :], xb.ap, reads=[xb])
                if t + 2 < NT:
                    xres_load(t + 2)
                mark_done("fin_done", NT, "snapXY")

            RF = None
            for n in range(4):
                wbb = wb_consume()
                if n == 3:
                    barrier(wait_state(state, "snapY_gmlp"))
                    RF = region_fin()
                    xres_load(0)
                    xres_load(1)
                for t in range(NT):
                    ensure(("ufin", pid, t))
                    acc = next_acc()
                    items = [mmf(acc.ap, yT[:, kc, t * 128:(t + 1) * 128], wbb.ap[:, kc, :], kc == 0, kc == KC - 1) for kc in range(KC)]
                    pe_group(items, reads=[yT_b[kc][t] for kc in range(KC)] + [wbb], writes=[acc])
                    if t == NT - 1:
                        wb_issue()
                    t_sq = act(junk.ap, acc.ap, AF.Square, reads=[acc], writes=[junk, oss], accum_out=oss.ap[:, t, n:n + 1])
                    cp("dve", hT[:, 4 * n:4 * n + 4, t * 128:(t + 1) * 128], acc.ap.rearrange("p (a b) -> p a b", a=4),
                       reads=[acc], writes=[hT_b[t]], waits=[t_sq])
                    if n == 3:
                        spawn(fin_chain(t, RF))
                    step_all()

        state["setup"] = P.snapshot()
        try:
            mod_compute(0, "ss")
            if stop == "mod":
                raise StopBuild()
            for l in range(L):
                for p in range(2):
                    run_pass(l, p)
        except StopBuild:
            pass
        flush()
        if stop is not None:
            x1_written[0] = dma("sp", "d_fin", y_d[0:128, :], xres[0].ap, reads=[xres[0]])
            fin_t = [P.op(e_, lambda e: e.memset(junk.ap[:, 0:8], 0.0), [P.snapshot()]) for e_ in ("dve",)]
            P.wait_only("sp", fin_t)
        P.wait_only("sp", [x1_written[t] for t in range(2 * NT)] + dbg_t)
        P.wait_only("pool", dbg_t)
        P.emit(block)
    return nc


_NC_CACHE = {}


def _kmajor(w2d):
    n = w2d.shape[1]
    return np.ascontiguousarray(w2d.reshape(KC, 128, n).transpose(1, 0, 2)).reshape(128, KC * n)


def _prep(inputs):
    f32 = np.float32
    x = np.asarray(inputs["x"], f32)
    c = np.asarray(inputs["c"], f32)
    positions = np.asarray(inputs["positions"], np.int32)
    w_mod = np.asarray(inputs["w_mod"], f32)
    b_mod = np.asarray(inputs["b_mod"], f32)
    g_pre = np.asarray(inputs["g_pre"], f32)
    g_post = np.asarray(inputs["g_post"], f32)
    w_in = np.asarray(inputs["w_in"], f32)
    w_out = np.asarray(inputs["w_out"], f32)
    swa_sinks = np.asarray(inputs["swa_sinks"], f32)
    sg_w = np.asarray(inputs["sg_w"], f32)
    sg_b = np.asarray(inputs["sg_b"], f32)
    sg_ln_g = np.asarray(inputs["sg_ln_g"], f32)
    sg_ln_b = np.asarray(inputs["sg_ln_b"], f32)
    gla_w_gate_up = np.asarray(inputs["gla_w_gate_up"], f32)
    gla_b_gate = np.asarray(inputs["gla_b_gate"], f32)
    gla_norm_g = np.asarray(inputs["gla_norm_g"], f32)

    ZO = 3600
    qcolsA = np.concatenate([np.arange(h * 64, (h + 1) * 64) for h in range(0, 8)])
    qcolsB = np.concatenate([np.arange(h * 64, (h + 1) * 64) for h in range(8, 16)])
    chunk_cols = {
        "cv": np.arange(3072, 3584), "cqk": np.arange(2560, 3072),
        "zc": np.arange(ZO + 1536, ZO + 2048), "za0": np.arange(ZO, ZO + 512), "za1": np.arange(ZO + 512, ZO + 1024),
        "kv": np.arange(1024, 1536), "qA": qcolsA, "qB": qcolsB,
        "zb": np.arange(ZO + 1024, ZO + 1536), "vs": np.arange(2048, 2560), "u": np.arange(1536, 2048),
    }
    win = np.empty((2 * 11, 128, KC * 512), f32)
    wing = np.empty((2, 128, KC * 16), f32)
    wout = np.empty((2 * 4, 128, KC * 512), f32)
    for l in range(2):
        for j, name in enumerate(CHUNKS):
            win[l * 11 + j] = _kmajor(w_in[l][:, chunk_cols[name]])
        wing[l] = _kmajor(w_in[l][:, 3584:3600])
        for n in range(4):
            wout[l * 4 + n] = _kmajor(w_out[l][:, n * 512:(n + 1) * 512])
    sgw = np.ascontiguousarray(sg_w.transpose(0, 3, 1, 2)).reshape(2, 128, 1024)
    sgb = np.ascontiguousarray(sg_b.transpose(2, 0, 1)).reshape(128, 16)
    rep = lambda v: np.ascontiguousarray(np.broadcast_to(v.reshape(1, -1), (128, v.size)))
    lng, lnb = rep(sg_ln_g), rep(sg_ln_b)
    wup = np.ascontiguousarray(np.concatenate([gla_w_gate_up, gla_b_gate[:, None, :]], axis=1).transpose(1, 0, 2)).reshape(17, 512)
    gng = rep(gla_norm_g)
    gpreT = np.ascontiguousarray(g_pre.reshape(2, KC, 128).transpose(2, 0, 1)).reshape(128, 32)
    gpost = rep(g_post)
    sinks = rep(swa_sinks)
    k = np.arange(128)[:, None]
    q = np.arange(128)[None, :]
    cst = np.concatenate([
        np.eye(128, dtype=f32), (k > q).astype(f32), (k <= q).astype(f32),
        np.where(k <= q, f32(-1.0 / 16), f32(0)).astype(f32), np.where(k > q, f32(-1.0 / 16), f32(0)).astype(f32)], axis=1)
    inv_freq = (np.float32(500000.0) ** (-(np.arange(8, dtype=f32) * f32(2.0 / 16)))).astype(f32)

    import os as _os1
    if _os1.environ.get("TINY") == "1":
        win = np.ascontiguousarray(win[0:2])
        wout = np.ascontiguousarray(wout[0:1])
    wmod = np.empty((24, 128, KC * 512), f32)
    for l in range(2):
        for ci in range(12):
            wmod[l * 12 + ci] = _kmajor(w_mod[l][:, ci * 512:(ci + 1) * 512])
    bmod = np.ascontiguousarray(b_mod.reshape(2, 1, 6144))
    shared = dict(win=win, wing=wing, wout=wout, sgw=sgw, sgb=sgb, lng=lng, lnb=lnb, wup=wup, gng=gng,
                  gpreT=gpreT, gpost=gpost, sinks=sinks, cst=np.ascontiguousarray(cst), wmod=wmod, bmod=bmod)
    maps = []
    for core in range(4):
        b = core
        m = dict(shared)
        m["x"] = np.ascontiguousarray(x[b])
        m["pos"] = np.ascontiguousarray(positions[b].reshape(16, 128).T)
        m["cT"] = np.ascontiguousarray(c[b].reshape(KC, 128).T)
        cst2 = np.zeros((128, 16), f32)
        cst2[:, 8:16] = inv_freq[None, :]
        m["cst2"] = cst2
        maps.append(m)
    return maps


def kernel(**inputs):
    if "nc" not in _NC_CACHE:
        _NC_CACHE["nc"] = build(2)
    nc = _NC_CACHE["nc"]
    maps = _prep(inputs)
    res = run_bass_kernel_spmd(nc, maps, core_ids=list(range(4)))
    out = np.empty((4, 2048, D), np.float32)
    for core in range(4):
        out[core] = np.asarray(res.results[core]["y"], np.float32)
    return out
```

```python
import contextlib
import math
import numpy as np
import concourse.bass as bass
import concourse.mybir as mybir
from concourse.bass_utils import run_bass_kernel_spmd

F32 = mybir.dt.float32
BF16 = mybir.dt.bfloat16
I32 = mybir.dt.int32
AF = mybir.ActivationFunctionType
ALU = mybir.AluOpType
AX = mybir.AxisListType

NT = 8
KC = 16
D = 2048
EPS = 1e-6
NWB = 2
PI = math.pi

Q_ORDER_A = [0, 2, 1, 3, 4, 6, 5, 7]
Q_ORDER_B = [8, 10, 9, 11, 12, 14, 13, 15]
CHUNKS = ["cv", "cqk", "zc", "za0", "za1", "kv", "qA", "qB", "zb", "vs", "u"]
Z_FC0 = {"za0": 0, "za1": 4, "zb": 8, "zc": 12}


class Prog:
    ENGS = ("pe", "act", "dve", "pool", "sp")

    def __init__(self, nc, stack):
        self.nc = nc
        self.stack = stack
        self.q = {e: [] for e in self.ENGS}
        self.sems = {}
        self.cnt = {}
        self.seen = {e: {} for e in self.ENGS}
        for e in ("pe", "act", "dve", "pool"):
            self.newsem(e)

    def newsem(self, key):
        self.sems[key] = self.stack.enter_context(self.nc.semaphore("s_" + key))
        self.cnt[key] = 0
        return key

    def _flat(self, waits, need):
        for t in waits:
            if t is None:
                continue
            if isinstance(t, (list, tuple)) and not (len(t) == 2 and isinstance(t[0], str)):
                self._flat(t, need)
            elif isinstance(t, dict):
                for k, v in t.items():
                    need[k] = max(need.get(k, 0), v)
            else:
                k, v = t
                need[k] = max(need.get(k, 0), v)

    def _waits(self, eng, waits):
        need = {}
        self._flat(waits, need)
        out = []
        for k, v in need.items():
            if self.seen[eng].get(k, 0) < v:
                self.seen[eng][k] = v
                out.append((k, v))
        return out

    def op(self, eng, fn, waits=(), inc=True):
        w = self._waits(eng, waits)
        ticket = None
        if inc:
            self.cnt[eng] += 1
            ticket = (eng, self.cnt[eng])
        self.q[eng].append((fn, w, eng if inc else None, 1))
        return ticket

    def next_ticket(self, eng):
        return (eng, self.cnt[eng] + 1)

    def dma(self, eng, semkey, out, in_, waits=()):
        if semkey not in self.sems:
            self.newsem(semkey)
        w = self._waits(eng, waits)
        self.cnt[semkey] += 16
        ticket = (semkey, self.cnt[semkey])
        self.q[eng].append((lambda e: e.dma_start(out=out, in_=in_), w, semkey, 16))
        return ticket

    def wait_only(self, eng, waits):
        w = self._waits(eng, waits)
        if w:
            self.q[eng].append((None, w, None, 0))

    def snapshot(self):
        return [(e, self.cnt[e]) for e in ("pe", "act", "dve", "pool") if self.cnt[e] > 0]

    def emit(self, block):
        sems = self.sems

        def run(engname):
            def body(e):
                for fn, w, inckey, incval in self.q[engname]:
                    for k, v in w:
                        e.wait_ge(sems[k], v)
                    if fn is None:
                        continue
                    ins = fn(e)
                    if inckey is not None:
                        ins.then_inc(sems[inckey], incval)
            return body

        block.tensor(run("pe"))
        block.scalar(run("act"))
        block.vector(run("dve"))
        block.gpsimd(run("pool"))
        block.sync(run("sp"))


class Buf:
    def __init__(self, ap):
        self.ap = ap
        self.w = None
        self.r = {}

    def rd(self):
        return [self.w]

    def wr(self):
        return [self.w, dict(self.r)]

    def did_read(self, t):
        if t is None:
            return
        k, v = t
        self.r[k] = max(self.r.get(k, 0), v)

    def did_write(self, t):
        self.w = t
        self.r = {}


def dtsize(dt):
    return 4 if dt in (F32, I32) else 2


class Arena:
    def __init__(self, ap, nbytes):
        self.ap = ap
        self.nbytes = nbytes
        self.off = 0

    def at(self, off, shape, dt, parts=128):
        n = 1
        for s in shape:
            n *= s
        sz = n * dtsize(dt)
        assert off % 4 == 0 and off + sz <= self.nbytes, (off, sz, self.nbytes)
        v = self.ap[0:parts, off // 2:(off + sz) // 2]
        if dt != BF16:
            v = v.bitcast(dt)
        if len(shape) == 2:
            v = v.rearrange("p (a b) -> p a b", a=shape[0])
        elif len(shape) == 3:
            v = v.rearrange("p (a b c) -> p a b c", a=shape[0], b=shape[1])
        elif len(shape) == 4:
            v = v.rearrange("p (a b c d) -> p a b c d", a=shape[0], b=shape[1], c=shape[2])
        return v

    def alloc(self, shape, dt, parts=128):
        n = 1
        for s in shape:
            n *= s
        sz = (n * dtsize(dt) + 31) // 32 * 32
        off = self.off
        self.off += sz
        assert self.off <= self.nbytes, ("arena overflow", self.off, self.nbytes)
        return self.at(off, shape, dt, parts)


class Sub:
    def __init__(self, arena, base, size):
        self.arena, self.base, self.size, self.off = arena, base, size, 0

    def alloc(self, shape, dt, parts=128):
        n = 1
        for s in shape:
            n *= s
        sz = (n * dtsize(dt) + 31) // 32 * 32
        off = self.base + self.off
        self.off += sz
        assert self.off <= self.size, ("region overflow", self.off, self.size)
        return self.arena.at(off, shape, dt, parts)


class BankPool:
    def __init__(self, bufs):
        self.free = list(bufs)

    def put(self, *bs):
        for b in bs:
            self.free.append(b)


def take(reqs):
    while True:
        if all(len(p.free) >= n for p, n in reqs):
            out = []
            for p, n in reqs:
                for _ in range(n):
                    out.append(p.free.pop(0))
            return out
        yield


class StopBuild(Exception):
    pass


def build(n_layers=2, dbg=False, stop=None):
    nc = bass.Bass("TRN2", target_bir_lowering=False)
    L = n_layers

    def din(name, shape, dt=F32):
        return nc.dram_tensor(name, shape, dt, kind="ExternalInput").ap()

    x_d = din("x", [2048, D])
    pos_d = din("pos", [128, 16], I32)
    cT_d = din("cT", [128, 16])
    wmod_d = din("wmod", [24, 128, KC * 512])
    bmod_d = din("bmod", [2, 1, 6144])
    gpreT_d = din("gpreT", [128, 32])
    gpost_d = din("gpost", [128, 2 * D])
    import os as _os0
    TINY = _os0.environ.get("TINY") == "1"
    win_d = din("win", [2 if TINY else 2 * 11, 128, KC * 512])
    wing_d = din("wing", [2, 128, KC * 16])
    wout_d = din("wout", [1 if TINY else 2 * 4, 128, KC * 512])
    sinks_d = din("sinks", [128, 32])
    sgw_d = din("sgw", [2, 128, 1024])
    sgb_d = din("sgb", [128, 16])
    lng_d = din("lng", [128, 1024])
    lnb_d = din("lnb", [128, 1024])
    wup_d = din("wup", [17, 512])
    gng_d = din("gng", [128, 256])
    cst_d = din("cst", [128, 640])
    cst2_d = din("cst2", [128, 16])
    y_d = nc.dram_tensor("y", [2048, D], F32, kind="ExternalOutput").ap()
    x1_d = nc.dram_tensor("x1s", [2048, D], F32).ap()
    dbg_t = []

    with contextlib.ExitStack() as st:
        P = Prog(nc, st)
        ARENA_BYTES = 212800
        arena_t = st.enter_context(nc.sbuf_tensor("arena", [128, ARENA_BYTES // 2], BF16))
        A = Arena(arena_t, ARENA_BYTES)
        ps_t = st.enter_context(nc.psum_tensor("ps", [128, 4096], F32))
        block = st.enter_context(nc.Block())

        def bank(i):
            return ps_t[:, i * 512:(i + 1) * 512]

        acc_b = [Buf(bank(0)), Buf(bank(1))]
        psb_ = []
        for k in range(6):
            b_ = Buf(bank(2 + k))
            b_.bf = bank(2 + k).bitcast(BF16)[:, 0:512]
            b_.bff = bank(2 + k).bitcast(BF16)
            psb_.append(b_)
        mixp = BankPool(psb_)
        tbp = mixp
        rr = {"acc": 0}

        def next_acc():
            rr["acc"] += 1
            return acc_b[rr["acc"] % 2]

        def op(eng, fn, reads=(), writes=(), waits=()):
            w = [b.rd() for b in reads] + [b.wr() for b in writes] + list(waits)
            t = P.op(eng, fn, w)
            for b in reads:
                b.did_read(t)
            for b in writes:
                b.did_write(t)
            return t

        def pe_group(items, reads=(), writes=(), waits=()):
            w = [b.rd() for b in reads] + [b.wr() for b in writes] + list(waits)
            n = len(items)
            t = None
            for i, fn in enumerate(items):
                last = i == n - 1
                t_ = P.op("pe", fn, w if i == 0 else (), inc=last)
                if last:
                    t = t_
            for b in reads:
                b.did_read(t)
            for b in writes:
                b.did_write(t)
            return t

        def dma(eng, key, out, in_, reads=(), writes=(), waits=()):
            w = [b.rd() for b in reads] + [b.wr() for b in writes] + list(waits)
            t = P.dma(eng, key, out, in_, w)
            for b in reads:
                b.did_read(t)
            for b in writes:
                b.did_write(t)
            return t

        def act(out, in_, func, reads=(), writes=(), waits=(), **kw):
            return op("act", lambda e: e.activation(out=out, in_=in_, func=func, **kw), reads, writes, waits)

        def tt(eng, out, in0, in1, alu, reads=(), writes=(), waits=()):
            return op(eng, lambda e: e.tensor_tensor(out=out, in0=in0, in1=in1, op=alu), reads, writes, waits)

        def ts(eng, out, in0, s1_, s2_, op0, op1=None, reads=(), writes=(), waits=()):
            if op1 is None:
                return op(eng, lambda e: e.tensor_scalar(out=out, in0=in0, scalar1=s1_, scalar2=None, op0=op0), reads, writes, waits)
            return op(eng, lambda e: e.tensor_scalar(out=out, in0=in0, scalar1=s1_, scalar2=s2_, op0=op0, op1=op1), reads, writes, waits)

        def stt(eng, out, in0, scalar, in1, op0, op1, reads=(), writes=(), waits=()):
            return op(eng, lambda e: e.scalar_tensor_tensor(out=out, in0=in0, scalar=scalar, in1=in1, op0=op0, op1=op1), reads, writes, waits)

        def cp(eng, out, in_, reads=(), writes=(), waits=()):
            if eng == "act":
                return act(out, in_, AF.Identity, reads, writes, waits)
            return op(eng, lambda e: e.tensor_copy(out=out, in_=in_), reads, writes, waits)

        def memset(eng, ap, val, writes=(), waits=()):
            return op(eng, lambda e: e.memset(ap, val), (), writes, waits)

        def mmf(out, lhsT, rhs, start=True, stop=True):
            return lambda e: e.matmul(out, lhsT=lhsT, rhs=rhs, start=start, stop=stop)

        def trf(out, in_):
            return lambda e: e.transpose(out=out, in_=in_, identity=identb.ap)

        def debug_dump(name, buf, ap, shape):
            if not dbg:
                return
            d = nc.dram_tensor("dbg_" + name, list(shape), F32, kind="ExternalOutput").ap()
            dbg_t.append(dma("pool", "d_dbg_" + name, d, ap, reads=list(buf)))

        gens = []

        def spawn(g):
            gens.append(g)

        def step_all():
            for g in list(gens):
                try:
                    next(g)
                except StopIteration:
                    gens.remove(g)

        def flush():
            n = 0
            while gens:
                step_all()
                n += 1
                assert n < 10000, "scheduler stuck"

        def take_now(reqs):
            n = 0
            while not all(len(p.free) >= k for p, k in reqs):
                step_all()
                n += 1
                assert n < 10000, "take_now stuck"
            out = []
            for p, k in reqs:
                for _ in range(k):
                    out.append(p.free.pop(0))
            return out

        def ensure(*keys):
            n = 0
            while not all(k in state for k in keys):
                step_all()
                n += 1
                assert n < 10000, ("ensure stuck", keys)

        def await_keys(*keys):
            while not all(k in state for k in keys):
                yield

        def wait_state(state, key):
            n = 0
            while key not in state:
                step_all()
                n += 1
                assert n < 10000, "wait_state stuck " + key
            return state.pop(key)

        hT = A.alloc([KC, 1024], BF16)
        yT = A.alloc([KC, 1024], BF16)
        hT_b = [Buf(hT[:, :, t * 128:(t + 1) * 128]) for t in range(NT)]
        yT_b = [[Buf(yT[:, fc, t * 128:(t + 1) * 128]) for t in range(NT)] for fc in range(KC)]
        WB = [Buf(A.alloc([KC, 512], BF16)) for _ in range(NWB)]
        xres = [Buf(A.alloc([D], F32)) for _ in range(2)]
        xnp = BankPool([Buf(A.alloc([D], BF16))])
        cst = Buf(A.alloc([384], F32))
        cst2 = Buf(A.alloc([16], F32))
        identb = Buf(A.alloc([128], BF16))
        trib = Buf(A.alloc([128], BF16))
        tripb = Buf(A.alloc([128], BF16))
        trip0b = Buf(A.alloc([128], BF16))
        smallc = Buf(A.alloc([8], F32))
        posi = Buf(A.alloc([16], I32))
        posf = Buf(A.alloc([16], F32))
        ang = Buf(A.alloc([16, 8], F32))
        ang2 = Buf(A.alloc([16, 8], F32))
        cosT = Buf(A.alloc([16, 8], F32))
        sinT = Buf(A.alloc([16, 8], F32))
        cTs = Buf(A.alloc([16], F32))
        scT = Buf(A.alloc([KC, 1], BF16))
        rowsb = Buf(A.alloc([512], F32, parts=1))
        onesr = Buf(A.alloc([128], F32, parts=1))
        GG1 = Buf(A.alloc([D], F32))
        GGs = [GG1, GG1]
        gpreT = Buf(A.alloc([2, KC], F32))
        GT = [Buf(A.alloc([KC], F32)) for _ in range(2)]
        shT = [Buf(A.alloc([KC], F32)) for _ in range(2)]
        sinks_sb = Buf(A.alloc([2, 16], F32))
        esink = Buf(A.alloc([2, 16], F32))
        sgWm = Buf(A.alloc([8, 128], BF16))
        sgb_sb = Buf(A.alloc([2, 8], F32))
        lng_sb = Buf(A.alloc([512], F32))
        lnb_sb = Buf(A.alloc([512], F32))
        wup_b = Buf(A.alloc([2, 256], BF16, parts=17))
        gng_sb = Buf(A.alloc([2, 128], F32))
        wing = Buf(A.alloc([2, KC, 16], BF16))
        cgT = Buf(A.alloc([1024], BF16, parts=32))
        Sst = Buf(A.alloc([4, 128], F32))
        ebl_all = Buf(A.alloc([NT, 4], F32))
        ssN = Buf(A.alloc([NT], F32))
        lnN = Buf(A.alloc([NT], F32))
        rstdN = Buf(A.alloc([NT], F32))
        oss = Buf(A.alloc([NT, 4], F32))
        ossum = Buf(A.alloc([NT], F32))
        rstdO = Buf(A.alloc([NT], F32))
        s1 = Buf(A.alloc([NT], F32))
        s2 = Buf(A.alloc([NT], F32))
        gstat = Buf(A.alloc([NT, 4], F32))
        ssg = Buf(A.alloc([NT, 4], F32))
        rstdg = Buf(A.alloc([NT, 4], F32))
        den = Buf(A.alloc([4, 8], F32))
        junk = Buf(A.alloc([512], BF16))
        halo = Buf(A.alloc([832], BF16))
        oc_one = Buf(A.alloc([4, 128], F32))
        ycp = BankPool([Buf(A.alloc([512], BF16)) for _ in range(2)])
        X_SIZE = 35840
        Y_SIZE = 20480
        X_BASE = A.off
        A.off += X_SIZE
        Y_BASE = A.off
        A.off += Y_SIZE
        assert A.off <= ARENA_BYTES, A.off
        xnp.put(Buf(A.at(Y_BASE + 12288, [D], BF16)))

        def region_gla_t():
            s = Sub(A, X_BASE, X_SIZE)
            r = {}
            r["Vg"] = [Buf(s.alloc([512], BF16)) for _ in range(NT)]
            r["sp"] = [Buf(s.alloc([256], F32)) for _ in range(NT)]
            r["gqk"] = BankPool([Buf(s.alloc([512], F32)) for _ in range(2)])
            r["eb"] = Buf(s.alloc([256], F32))
            r["enb"] = Buf(s.alloc([256], F32))
            r["ec"] = Buf(s.alloc([256], F32))
            r["qkd"] = BankPool([Buf(s.alloc([3, 256], BF16)) for _ in range(3)])
            r["AT"] = BankPool([Buf(s.alloc([4, 128], BF16)) for _ in range(2)])
            r["keT"] = BankPool([Buf(s.alloc([4, 128], BF16)) for _ in range(2)])
            r["Sbfp"] = BankPool([Buf(s.alloc([4, 128], BF16)) for _ in range(3)])
            return r

        def region_swa():
            s = Sub(A, X_BASE, X_SIZE)
            r = {}
            kT = s.alloc([4, 1024], BF16)
            r["kT"] = kT
            r["kT_b"] = [Buf(kT[:, :, t * 128:(t + 1) * 128]) for t in range(NT)]
            r["Vb"] = [Buf(s.alloc([4, 80], BF16)) for _ in range(NT)]
            r["PT"] = BankPool([Buf(s.alloc([4, 128], BF16)) for _ in range(8)])
            r["qtmp"] = [Buf(s.alloc([8, 64], F32)) for _ in range(1)]
            r["qr"] = BankPool([Buf(s.alloc([8, 64], BF16)) for _ in range(2)])
            r["qT"] = BankPool([Buf(s.alloc([8, 128], BF16)) for _ in range(2)])
            r["ktmp"] = [Buf(s.alloc([4, 64], F32)) for _ in range(1)]
            r["kr"] = BankPool([Buf(s.alloc([4, 64], BF16)) for _ in range(2)])
            r["ya"] = BankPool([Buf(s.alloc([8, 64], BF16)) for _ in range(2)])
            r["ra"] = [Buf(s.alloc([8, 8], F32)) for _ in range(4)]
            return r


        def region_gla_p():
            s = Sub(A, Y_BASE, Y_SIZE)
            r = {"op": [Buf(s.alloc([4, 128], F32)) for _ in range(NT)]}
            r["qeTp"] = BankPool([Buf(s.alloc([4, 128], BF16)) for _ in range(3)])
            return r

        def region_gmlp():
            s = Sub(A, Y_BASE, Y_SIZE)
            r = {}
            r["gv"] = BankPool([Buf(s.alloc([512], F32)) for _ in range(2)])
            r["vn"] = [Buf(s.alloc([512], BF16)) for _ in range(NT)]
            r["gu"] = BankPool([Buf(s.alloc([512], BF16)) for _ in range(2)])
            r["yb"] = BankPool([Buf(s.alloc([512], BF16)) for _ in range(2)])
            r["tmx"] = [Buf(s.alloc([8, 64], F32)) for _ in range(1)]
            return r

        def region_fin():
            s = Sub(A, Y_BASE, Y_SIZE)
            return {"tmp": [Buf(s.alloc([D], F32)) for _ in range(1)]}

        def barrier(snap):
            for e in ("pe", "act", "dve", "pool", "sp"):
                P.wait_only(e, [snap, list(dbg_t)])

        stream = []
        for ci in range(8):
            stream.append((wmod_d[ci], 512))
        for l in range(L):
            for p in range(2):
                for j, name in enumerate(CHUNKS):
                    stream.append((win_d[l * 11 + j], 512))
                    if name == "kv" and p == 0:
                        for ci in range(8, 12):
                            stream.append((wmod_d[l * 12 + ci], 512))
                    if name == "kv" and p == 1 and l + 1 < L:
                        for ci in range(8):
                            stream.append((wmod_d[(l + 1) * 12 + ci], 512))
                for n in range(4):
                    stream.append((wout_d[l * 4 + n], 512))
        st_state = {"issued": 0, "consumed": 0}

        def wb_issue():
            k = st_state["issued"]
            if k >= len(stream):
                return
            src, ncols = stream[k]
            b = WB[k % NWB]
            dma("pool", f"d_wb{k % NWB}", b.ap[:, :, 0:ncols], src.rearrange("p (k n) -> p k n", k=KC), writes=[b])
            st_state["issued"] += 1

        def wb_consume():
            k = st_state["consumed"]
            assert k < st_state["issued"], (k, st_state)
            st_state["consumed"] += 1
            return WB[k % NWB]

        for _ in range(NWB):
            wb_issue()
        SX = Sub(A, X_BASE, X_SIZE)
        cstA = Buf(SX.alloc([256], F32))
        wup_f = Buf(SX.alloc([2, 256], F32, parts=17))
        dma("sp", "d_c0", cst.ap, cst_d[:, 256:640], writes=[cst])
        dma("sp", "d_c0a", cstA.ap, cst_d[:, 0:256], writes=[cstA])
        dma("sp", "d_c1", cst2.ap, cst2_d, writes=[cst2])
        dma("sp", "d_c2", posi.ap, pos_d, writes=[posi])
        dma("sp", "d_c3", cTs.ap, cT_d, writes=[cTs])
        dma("sp", "d_c7", gpreT.ap, gpreT_d.rearrange("p (a b) -> p a b", a=2), writes=[gpreT])
        dma("sp", "d_c8", sinks_sb.ap, sinks_d.rearrange("p (a b) -> p a b", a=2), writes=[sinks_sb])
        dma("sp", "d_c9", sgb_sb.ap, sgb_d.rearrange("p (a b) -> p a b", a=2), writes=[sgb_sb])
        dma("sp", "d_c12", wup_f.ap, wup_d.rearrange("p (a b) -> p a b", a=2), writes=[wup_f])
        dma("sp", "d_c13", gng_sb.ap, gng_d.rearrange("p (a b) -> p a b", a=2), writes=[gng_sb])
        dma("pool", "d_c14", wing.ap, wing_d.rearrange("l p (k n) -> p l k n", k=KC), writes=[wing])

        identf = cstA.ap[:, 0:128]
        tripf = cstA.ap[:, 128:256]
        trif = cst.ap[:, 0:128]
        ntri16 = cst.ap[:, 128:256]
        nup16 = cst.ap[:, 256:384]
        invf = cst2.ap[:, 8:16]
        cp("dve", identb.ap, identf, reads=[cstA], writes=[identb])
        cp("dve", trib.ap, trif, reads=[cst], writes=[trib])
        cp("dve", tripb.ap, tripf, reads=[cstA], writes=[tripb])
        memset("dve", trip0b.ap, 0.0, writes=[trip0b])
        memset("dve", onesr.ap, 1.0, writes=[onesr])
        memset("dve", smallc.ap[:, 0:1], -1.0 / 16.0, writes=[smallc])
        memset("dve", smallc.ap[:, 1:2], EPS, writes=[smallc])
        memset("dve", smallc.ap[:, 2:3], -PI, writes=[smallc])
        nsix = smallc.ap[:, 0:1]
        epsc = smallc.ap[:, 1:2]
        negpi = smallc.ap[:, 2:3]
        memset("dve", cgT.ap, 1.0, writes=[cgT])
        cp("dve", wup_b.ap, wup_f.ap, reads=[wup_f], writes=[wup_b])
        cp("dve", posf.ap, posi.ap, reads=[posi], writes=[posf])
        tt("dve", ang.ap, posf.ap.unsqueeze(2).to_broadcast([128, 16, 8]), invf.unsqueeze(1).to_broadcast([128, 16, 8]),
           ALU.mult, reads=[posf, cst2], writes=[ang])
        ts("dve", ang2.ap, ang.ap, 0.5 * PI, None, ALU.add, reads=[ang], writes=[ang2])
        MAGIC = 8388608.0
        for src, dst in ((ang, sinT), (ang2, cosT)):
            ts("dve", dst.ap, src.ap, 1.0 / (2 * PI), None, ALU.mult, reads=[src], writes=[dst])
            ts("dve", dst.ap, dst.ap, MAGIC, None, ALU.add, writes=[dst])
            ts("dve", dst.ap, dst.ap, -MAGIC, None, ALU.add, writes=[dst])
            stt("dve", src.ap, dst.ap, -2 * PI, src.ap, ALU.mult, ALU.add, reads=[dst], writes=[src])
            ts("dve", src.ap, src.ap, -PI, PI, ALU.max, ALU.min, writes=[src])
            act(dst.ap, src.ap, AF.Sin, reads=[src], writes=[dst])
        act(esink.ap, sinks_sb.ap, AF.Exp, reads=[sinks_sb], writes=[esink])
        act(scT.ap, cTs.ap.rearrange("p (k b) -> p k b", k=KC), AF.Silu, reads=[cTs], writes=[scT])

        modT_bank = {}

        def mod_compute(l, part):
            chunks = range(8) if part == "ss" else range(8, 12)
            if part == "ss":
                (mt,) = take_now([(mixp, 1)])
            else:
                dma("sp", "d_gp", GGs[l].ap, gpost_d[:, l * D:(l + 1) * D], writes=[GGs[l]])
            for ci in chunks:
                wbb = wb_consume()
                (mb,) = take_now([(mixp, 1)])
                dma("sp", "d_bm", rowsb.ap, bmod_d[l][:, ci * 512:(ci + 1) * 512], writes=[rowsb])
                items = [mmf(mb.ap[0:1, :], scT.ap[:, kc, :], wbb.ap[:, kc, :], kc == 0, kc == KC - 1) for kc in range(KC)]
                pe_group(items, reads=[scT, wbb], writes=[mb])
                wb_issue()
                tt("dve", rowsb.ap, mb.ap[0:1, :], rowsb.ap, ALU.add, reads=[mb], writes=[rowsb])
                mixp.put(mb)
                if part == "ss":
                    items = [mmf(mt.ap[:, ci * 4 + j: ci * 4 + j + 1], rowsb.ap[0:1, j * 128:(j + 1) * 128], onesr.ap[0:1, 0:1]) for j in range(4)]
                    pe_group(items, reads=[rowsb, onesr], writes=[mt])
                else:
                    (gb,) = take_now([(mixp, 1)])
                    pe_group([mmf(gb.ap, onesr.ap[0:1, :], rowsb.ap[0:1, :])], reads=[rowsb, onesr], writes=[gb])
                    c0 = (ci - 8) * 512
                    tt("dve", GGs[l].ap[:, c0:c0 + 512], gb.ap, GGs[l].ap[:, c0:c0 + 512], ALU.mult, reads=[gb], writes=[GGs[l]])
                    mixp.put(gb)
                step_all()
            if part == "ss":
                cp("dve", shT[l].ap, mt.ap[:, 0:16], reads=[mt], writes=[shT[l]])
                stt("dve", GT[l].ap, mt.ap[:, 16:32], 1.0, gpreT.ap[:, l, :], ALU.add, ALU.mult,
                    reads=[mt, gpreT], writes=[GT[l]])
                mixp.put(mt)

        import os as _os
        PN_LEVEL = int(_os.environ.get("PN_LEVEL", "9"))

        def phase_n(l, t, xb, xn, pid):
            if PN_LEVEL == 0:
                xnp.put(xn)
                state["hT", pid, t] = True
                yield
                return
            if PN_LEVEL != 15:
                memset("dve", ssN.ap[:, t:t + 1], 0.0, writes=[ssN])
            if PN_LEVEL == 14:
                pass
            elif PN_LEVEL == 12:
                act(xn.ap, xb.ap, AF.Identity, reads=[xb], writes=[xn, ssN])
            elif PN_LEVEL in (13, 15):
                cp("dve", xn.ap, xb.ap, reads=[xb], writes=[xn])
            else:
                act(xn.ap, xb.ap, AF.Square, reads=[xb], writes=[xn, ssN], accum_out=ssN.ap[:, t:t + 1])
            if PN_LEVEL in (10, 12, 13, 14, 15):
                xnp.put(xn)
                state["hT", pid, t] = True
                yield
                return
            act(lnN.ap[:, t:t + 1], ssN.ap[:, t:t + 1], AF.Ln, reads=[ssN, smallc], writes=[lnN], scale=1.0 / D, bias=epsc)
            act(rstdN.ap[:, t:t + 1], lnN.ap[:, t:t + 1], AF.Exp, reads=[lnN], writes=[rstdN], scale=-0.5)
            if PN_LEVEL == 11:
                xnp.put(xn)
                state["hT", pid, t] = True
                yield
                return
            ts("dve", xn.ap, xb.ap, rstdN.ap[:, t:t + 1], None, ALU.mult, reads=[xb, rstdN], writes=[xn])
            yield
            if PN_LEVEL == 1:
                xnp.put(xn)
                state["hT", pid, t] = True
                yield
                return
            for u in range(4):
                (tb,) = yield from take([(tbp, 1)])
                items = [trf(tb.bf[:, i * 128:(i + 1) * 128], xn.ap[:, (u * 4 + i) * 128:(u * 4 + i + 1) * 128]) for i in range(4)]
                pe_group(items, reads=[xn, identb], writes=[tb])
                if u == 3:
                    xnp.put(xn)
                yield
                for i in range(4 if PN_LEVEL >= 3 else 0):
                    kc = u * 4 + i
                    if u % 2 == 0:
                        act(hT[:, kc, t * 128:(t + 1) * 128], tb.bf[:, i * 128:(i + 1) * 128], AF.Identity,
                            reads=[tb, GT[l], shT[l]], writes=[hT_b[t]],
                            scale=GT[l].ap[:, kc:kc + 1], bias=shT[l].ap[:, kc:kc + 1])
                    else:
                        ts("dve", hT[:, kc, t * 128:(t + 1) * 128], tb.bf[:, i * 128:(i + 1) * 128],
                           GT[l].ap[:, kc:kc + 1], shT[l].ap[:, kc:kc + 1], ALU.mult, ALU.add,
                           reads=[tb, GT[l], shT[l]], writes=[hT_b[t]])
                tbp.put(tb)
            state["hT", pid, t] = True
            if dbg and l == 0 and t == NT - 1:
                debug_dump("hT", hT_b, hT, [128, KC, 1024])

        def tok_group(wbb, t, pid):
            ensure(("hT", pid, t))
            acc = next_acc()
            items = [mmf(acc.ap, hT[:, kc, t * 128:(t + 1) * 128], wbb.ap[:, kc, :], kc == 0, kc == KC - 1) for kc in range(KC)]
            pe_group(items, reads=[hT_b[t], wbb], writes=[acc])
            return acc

        def feat_group(wbuf, lhs, hf, pid, ncol=128):
            ensure(*[("hT", pid, 4 * hf + i) for i in range(4)])
            acc = next_acc()
            items = [mmf(acc.ap[0:ncol, :], lhs(kc), hT[:, kc, hf * 512:(hf + 1) * 512], kc == 0, kc == KC - 1) for kc in range(KC)]
            pe_group(items, reads=[hT_b[4 * hf + i] for i in range(4)] + [wbuf], writes=[acc])
            return acc

        x1_written = [None] * (2 * NT)
        state = {}
        rr_den = [0]

        def mark_done(key, n, snapname):
            state[key] = state.get(key, 0) + 1
            if state[key] == n:
                del state[key]
                state[snapname] = P.snapshot()

        def run_pass(l, p):
            pid = 2 * l + p
            src_d = x_d if l == 0 else x1_d
            dst_d = x1_d if l + 1 < L else y_d
            if pid > 0:
                barrier(wait_state(state, "snapXY"))
            else:
                barrier(state.pop("setup"))
            RT = region_gla_t()
            RP = region_gla_p()
            if p == 0:
                memset("dve", Sst.ap, 0.0, writes=[Sst])
            sb0 = RT["Sbfp"].free.pop(0)
            cp("dve", sb0.ap[0:64], Sst.ap[0:64], reads=[Sst], writes=[sb0])
            state["Sbf", pid, 0] = sb0

            def pn_all():
                def load(t):
                    b = xres[t % 2]
                    w = [x1_written[p * NT + t]] if l > 0 else []
                    dma("sp", f"d_xr{t % 2}", b.ap, src_d[p * 1024 + t * 128: p * 1024 + (t + 1) * 128, :], writes=[b], waits=w)
                load(0)
                load(1)
                for t in range(NT):
                    (xn,) = yield from take([(xnp, 1)])
                    pn = phase_n(l, t, xres[t % 2], xn, pid)
                    next(pn)
                    if t + 2 < NT:
                        load(t + 2)
                    spawn(pn)
                    yield

            spawn(pn_all())
            for _ in range(6):
                step_all()
            for b_ in (oss, s1, s2, ssg):
                memset("dve", b_.ap, 0.0, writes=[b_])
            dma("sp", "d_lng", lng_sb.ap, lng_d[:, l * 512:(l + 1) * 512], writes=[lng_sb])
            dma("sp", "d_lnb", lnb_sb.ap, lnb_d[:, l * 512:(l + 1) * 512], writes=[lnb_sb])

            for hf in range(2):
                acc = feat_group(wing, lambda kc: wing.ap[:, l, kc, :], hf, pid, ncol=16)
                cp("act", cgT.ap[0:16, hf * 512:(hf + 1) * 512], acc.ap[0:16, :], reads=[acc], writes=[cgT])
                step_all()

            def sp_chain(t):
                (mb,) = yield from take([(mixp, 1)])
                pe_group([mmf(mb.ap[:, 0:256], cgT.ap[0:17, t * 128:(t + 1) * 128], wup_b.ap[0:17, l, :])],
                         reads=[cgT, wup_b], writes=[mb])
                yield
                spb = RT["sp"][t]
                act(spb.ap, mb.ap[:, 0:256], AF.Exp, reads=[mb], writes=[spb], scale=-1.0)
                mixp.put(mb)
                act(spb.ap, spb.ap, AF.Ln, writes=[spb], bias=1.0, scale=1.0)
                state["sp", pid, t] = True

            def gla_chain(t, acc, gq):
                cp("dve", gq.ap, acc.ap, reads=[acc], writes=[gq])
                if l == 0 and t == 1:
                    debug_dump("gq", [gq], gq.ap, [128, 512])
                yield
                yield from await_keys(("sp", pid, t))
                spb = RT["sp"][t]
                bc, bl, qkd = yield from take([(mixp, 2), (RT["qkd"], 1)])
                pe_group([mmf(bc.ap[:, 0:256], ntri16, spb.ap), mmf(bc.ap[:, 256:512], nup16, spb.ap)],
                         reads=[cst, spb], writes=[bc])
                pe_group([mmf(bl.ap[0:64, h:h + 1], spb.ap[:, h * 64:(h + 1) * 64], nsix) for h in range(4)],
                         reads=[spb, smallc], writes=[bl])
                yield
                eb, enb, ec = RT["eb"], RT["enb"], RT["ec"]
                act(eb.ap, bc.ap[:, 0:256], AF.Exp, reads=[bc], writes=[eb])
                act(enb.ap, bc.ap[:, 0:256], AF.Exp, reads=[bc], writes=[enb], scale=-1.0)
                act(ec.ap, bc.ap[:, 256:512], AF.Exp, reads=[bc], writes=[ec])
                act(ebl_all.ap[0:64, t, :], bl.ap[0:64, 0:4], AF.Exp, reads=[bl], writes=[ebl_all])
                mixp.put(bc, bl)
                qe, ke, kd = qkd.ap[:, 0, :], qkd.ap[:, 1, :], qkd.ap[:, 2, :]
                stt("dve", qe, gq.ap[:, 0:256], 0.125, eb.ap, ALU.mult, ALU.mult, reads=[gq, eb], writes=[qkd])
                tt("dve", ke, gq.ap[:, 256:512], enb.ap, ALU.mult, reads=[gq, enb], writes=[qkd])
                tt("dve", kd, gq.ap[:, 256:512], ec.ap, ALU.mult, reads=[gq, ec], writes=[qkd])
                RT["gqk"].put(gq)
                yield
                reqs = [(mixp, 3), (RT["keT"], 1), (RP["qeTp"], 1)] + ([(RT["Sbfp"], 1)] if t + 1 < NT else [])
                got = yield from take(reqs)
                tbq, tbk, ds, keT, qeT = got[0:5]
                sb_next = got[5] if t + 1 < NT else None
                pe_group([trf(tbq.bf[0:64, h * 128:(h + 1) * 128], qe[:, h * 64:(h + 1) * 64]) for h in range(4)],
                         reads=[qkd, identb], writes=[tbq])
                pe_group([trf(tbk.bf[0:64, h * 128:(h + 1) * 128], ke[:, h * 64:(h + 1) * 64]) for h in range(4)],
                         reads=[qkd, identb], writes=[tbk])
                Vg = RT["Vg"][t]
                pe_group([mmf(ds.ap[0:64, h * 128:(h + 1) * 128], kd[:, h * 64:(h + 1) * 64], Vg.ap[:, h * 128:(h + 1) * 128]) for h in range(4)],
                         reads=[qkd, Vg], writes=[ds])
                RT["qkd"].put(qkd)
                yield
                if t > 0:
                    yield from await_keys(("Sdone", pid, t - 1))
                cp("dve", qeT.ap[0:64], tbq.bf[0:64, :].rearrange("p (a b) -> p a b", a=4), reads=[tbq], writes=[qeT])
                cp("dve", keT.ap[0:64], tbk.bf[0:64, :].rearrange("p (a b) -> p a b", a=4), reads=[tbk], writes=[keT])
                mixp.put(tbq, tbk)
                for h in range(4):
                    stt("dve", Sst.ap[0:64, h, :], Sst.ap[0:64, h, :], ebl_all.ap[0:64, t, h:h + 1], ds.ap[0:64, h * 128:(h + 1) * 128],
                        ALU.mult, ALU.add, reads=[ebl_all, ds], writes=[Sst])
                mixp.put(ds)
                if t + 1 < NT:
                    cp("dve", sb_next.ap[0:64], Sst.ap[0:64], reads=[Sst], writes=[sb_next])
                    state["Sbf", pid, t + 1] = sb_next
                state["Sdone", pid, t] = True
                yield
                sc, AT = yield from take([(mixp, 1), (RT["AT"], 1)])
                pe_group([mmf(sc.ap[:, h * 128:(h + 1) * 128], keT.ap[0:64, h, :], qeT.ap[0:64, h, :]) for h in range(4)],
                         reads=[keT, qeT], writes=[sc])
                RT["keT"].put(keT)
                yield
                tt("dve", AT.ap, sc.ap.rearrange("p (a b) -> p a b", a=4), trib.ap.unsqueeze(1).to_broadcast([128, 4, 128]),
                   ALU.mult, reads=[sc, trib], writes=[AT])
                mixp.put(sc)
                yield
                (ob,) = yield from take([(mixp, 1)])
                Sb = state["Sbf", pid, t]
                items = []
                for h in range(4):
                    items.append(mmf(ob.ap[:, h * 128:(h + 1) * 128], AT.ap[:, h, :], Vg.ap[:, h * 128:(h + 1) * 128], True, False))
                    items.append(mmf(ob.ap[:, h * 128:(h + 1) * 128], qeT.ap[0:64, h, :], Sb.ap[0:64, h, :], False, True))
                pe_group(items, reads=[AT, Vg, qeT, Sb], writes=[ob])
                RT["AT"].put(AT)
                RP["qeTp"].put(qeT)
                RT["Sbfp"].put(Sb)
                yield
                cp("act", RP["op"][t].ap, ob.ap.rearrange("p (a b) -> p a b", a=4), reads=[ob], writes=[RP["op"][t]])
                mixp.put(ob)
                state["op", pid, t] = True
                mark_done("gla_done", NT, "snapX_glat")

            def corr_chain(t):
                yield from await_keys(("op", pid, t))
                (yc,) = yield from take([(ycp, 1)])
                oc = oc_one
                op_b = RP["op"][t]
                for h in range(4):
                    act(junk.ap[:, 0:128], op_b.ap[:, h, :], AF.Square, reads=[op_b], writes=[junk, ssg], accum_out=ssg.ap[:, t, h:h + 1])
                act(rstdg.ap[:, t, :], ssg.ap[:, t, :], AF.Ln, reads=[ssg, smallc], writes=[rstdg], scale=1.0 / 128, bias=epsc)
                act(rstdg.ap[:, t, :], rstdg.ap[:, t, :], AF.Exp, writes=[rstdg], scale=-0.5)
                tt("dve", oc.ap, op_b.ap, rstdg.ap[:, t, :].unsqueeze(2).to_broadcast([128, 4, 128]), ALU.mult,
                   reads=[op_b, rstdg], writes=[oc])
                tt("dve", yc.ap.rearrange("p (a b) -> p a b", a=4), oc.ap,
                   gng_sb.ap[:, l, :].unsqueeze(1).to_broadcast([128, 4, 128]), ALU.mult, reads=[oc, gng_sb], writes=[yc])
                yield
                (tb,) = yield from take([(tbp, 1)])
                pe_group([trf(tb.bf[:, i * 128:(i + 1) * 128], yc.ap[:, i * 128:(i + 1) * 128]) for i in range(4)],
                         reads=[yc, identb], writes=[tb])
                ycp.put(yc)
                yield
                yv = yT[:, 12:16, t * 128:(t + 1) * 128]
                tt("dve", yv, yv, tb.bf.rearrange("p (a b) -> p a b", a=4), ALU.mult,
                   reads=[tb], writes=[yT_b[12 + i][t] for i in range(4)])
                tbp.put(tb)
                mark_done("corr_done", NT, "snapY_glap")

            def z_chunk(name):
                wbb = wb_consume()
                fc0 = Z_FC0[name]
                for m in range(4):
                    for hf in range(2):
                        acc = feat_group(wbb, lambda kc, m=m: wbb.ap[:, kc, m * 128:(m + 1) * 128], hf, pid)
                        if m == 3 and hf == 1:
                            wb_issue()
                        act(yT[:, fc0 + m, hf * 512:(hf + 1) * 512], acc.ap, AF.Silu, reads=[acc],
                            writes=[yT_b[fc0 + m][4 * hf + i] for i in range(4)])
                        step_all()
                if stop == name:
                    raise StopBuild()

            def rope4(src, nh, t, ra):
                C = cosT.ap[:, p * NT + t, :].unsqueeze(1).to_broadcast([128, nh, 8])
                S = sinT.ap[:, p * NT + t, :].unsqueeze(1).to_broadcast([128, nh, 8])
                k1, k2 = src.ap[:, :, 0:8], src.ap[:, :, 8:16]
                vs_ = [r_.ap[:, 0:nh, :] for r_ in ra]
                tt("dve", vs_[0], k1, C, ALU.mult, reads=[src, cosT], writes=[ra[0]])
                tt("dve", vs_[1], k2, S, ALU.mult, reads=[src, sinT], writes=[ra[1]])
                tt("dve", vs_[2], k2, C, ALU.mult, reads=[src, cosT], writes=[ra[2]])
                tt("dve", vs_[3], k1, S, ALU.mult, reads=[src, sinT], writes=[ra[3]])
                return vs_

            def kv_chain(t, acc, RS, kr):
                Vb = RS["Vb"][t]
                cp("act", Vb.ap[:, :, 0:64], acc.ap[:, 256:512].rearrange("p (a b) -> p a b", a=4), reads=[acc], writes=[Vb])
                ktmp = RS["ktmp"][0]
                cp("act", ktmp.ap, acc.ap[:, 0:256].rearrange("p (a b) -> p a b", a=4), reads=[acc], writes=[ktmp])
                KV_LEVEL = int(_os.environ.get("KV_LEVEL", "9"))
                if KV_LEVEL == 0:
                    RS["kr"].put(kr)
                    state["kT", pid, t] = True
                    yield
                    return
                ra = RS["ra"]
                av, bv, cv_, dv = rope4(ktmp, 4, t, ra)
                tt("dve", kr.ap[:, :, 0:8], av, bv, ALU.subtract, reads=[ra[0], ra[1]], writes=[kr])
                tt("dve", kr.ap[:, :, 8:16], cv_, dv, ALU.add, reads=[ra[2], ra[3]], writes=[kr])
                cp("dve", kr.ap[:, :, 16:64], ktmp.ap[:, :, 16:64], reads=[ktmp], writes=[kr])
                yield
                if KV_LEVEL == 1:
                    RS["kr"].put(kr)
                    state["kT", pid, t] = True
                    return
                (tb,) = yield from take([(tbp, 1)])
                pe_group([trf(tb.bf[0:64, g * 128:(g + 1) * 128], kr.ap[:, g, :]) for g in range(4)],
                         reads=[kr, identb], writes=[tb])
                RS["kr"].put(kr)
                if l == 0 and t == 1:
                    debug_dump("kr", [kr], kr.ap, [128, 4, 2, 64])
                    debug_dump("Vb", [Vb], Vb.ap, [128, 4, 65])
                yield
                cp("act", RS["kT_b"][t].ap[0:64], tb.bf[0:64, :].rearrange("p (a b) -> p a b", a=4), reads=[tb], writes=[RS["kT_b"][t]])
                tbp.put(tb)
                state["kT", pid, t] = True
                if t == NT - 1 and p == 0 and KV_LEVEL >= 3:
                    cp("dve", halo.ap[0:64, 0:512].rearrange("p (a b) -> p a b", a=4), RS["kT_b"][t].ap[0:64], reads=[RS["kT_b"][t]], writes=[halo])
                    cp("dve", halo.ap[:, 512:832].rearrange("p (a b) -> p a b", a=4)[:, :, 0:65], Vb.ap[:, :, 0:65], reads=[Vb], writes=[halo])
                    state["halo", pid + 1] = True

            def q_chain(c, t, acc, RS, qr):
                qtmp = RS["qtmp"][0]
                cp("act", qtmp.ap, acc.ap.rearrange("p (a b) -> p a b", a=8), reads=[acc], writes=[qtmp])
                ra = RS["ra"]
                av, bv, cv_, dv = rope4(qtmp, 8, t, ra)
                tt("dve", qr.ap[:, :, 0:8], av, bv, ALU.subtract, reads=[ra[0], ra[1]], writes=[qr])
                tt("dve", qr.ap[:, :, 8:16], cv_, dv, ALU.add, reads=[ra[2], ra[3]], writes=[qr])
                cp("dve", qr.ap[:, :, 16:64], qtmp.ap[:, :, 16:64], reads=[qtmp], writes=[qr])
                yield
                tb, qT = yield from take([(tbp, 1), (RS["qT"], 1)])
                pe_group([trf(tb.bff[0:64, i * 128:(i + 1) * 128], qr.ap[:, i, :]) for i in range(8)],
                         reads=[qr, identb], writes=[tb])
                RS["qr"].put(qr)
                if l == 0 and t == 1 and c == 0:
                    debug_dump("qr", [qr], qr.ap, [128, 8, 64])
                yield
                cp("dve", qT.ap[0:64], tb.bff[0:64, :].rearrange("p (a b) -> p a b", a=8), reads=[tb], writes=[qT])
                tbp.put(tb)
                yield
                ya = None
                has_prev = not (p == 0 and t == 0)
                if has_prev:
                    yield from await_keys(("kT", pid, t), ("halo", pid) if t == 0 else ("kT", pid, t - 1))
                else:
                    yield from await_keys(("kT", pid, t))
                kbs = [0, 1] if has_prev else [1]
                for gl in range(2):
                    g = 2 * c + gl
                    sc0, sc1, pt0, pt1 = yield from take([(mixp, 2), (RS["PT"], 2)])
                    scs, pts = [sc0, sc1], [pt0, pt1]
                    for kb in kbs:
                        sc = scs[kb]
                        if kb == 0 and t == 0:
                            kTv = halo.ap[:, 0:512].rearrange("p (a b) -> p a b", a=4)[:, g, :]
                            kdep = [halo]
                        else:
                            tk = t - 1 + kb
                            kTv = RS["kT"][:, g, tk * 128:(tk + 1) * 128]
                            kdep = [RS["kT_b"][tk]]
                        items = [mmf(sc.ap.rearrange("p (a b) -> p a b", a=4), kTv[0:64, :], qT.ap[0:64, 4 * gl:4 * gl + 4, :])]
                        pe_group(items, reads=kdep + [qT], writes=[sc])
                    if gl == 1:
                        RS["qT"].put(qT)
                    yield
                    for kb in kbs:
                        PTb = pts[kb]
                        act(PTb.ap, scs[kb].ap.rearrange("p (a b) -> p a b", a=4), AF.Exp, reads=[scs[kb]], writes=[PTb], scale=0.125)
                        mask = tripb if kb == 0 else trib
                        tt("dve", PTb.ap, PTb.ap, mask.ap.unsqueeze(1).to_broadcast([128, 4, 128]), ALU.mult, reads=[mask], writes=[PTb])
                    mixp.put(*scs)
                    yield
                    if gl == 0:
                        pvb, ya = yield from take([(mixp, 1), (RS["ya"], 1)])
                    else:
                        (pvb,) = yield from take([(mixp, 1)])
                    pv = pvb.ap[:, 0:260].rearrange("p (a b) -> p a b", a=4)
                    items = []
                    for hh in range(4):
                        if t == 0:
                            Vp = halo.ap[:, 512:832].rearrange("p (a b) -> p a b", a=4)[:, g, 0:65]
                        elif has_prev:
                            Vp = RS["Vb"][t - 1].ap[:, g, 0:65]
                        Vc = RS["Vb"][t].ap[:, g, 0:65]
                        if has_prev:
                            items.append(mmf(pv[:, hh, :], pts[0].ap[:, hh, :], Vp, True, False))
                            items.append(mmf(pv[:, hh, :], pts[1].ap[:, hh, :], Vc, False, True))
                        else:
                            items.append(mmf(pv[:, hh, :], pts[1].ap[:, hh, :], Vc, True, True))
                    rd = [pts[1], RS["Vb"][t]]
                    if has_prev:
                        rd += [pts[0]] + ([halo] if t == 0 else [RS["Vb"][t - 1]])
                    pe_group(items, reads=rd, writes=[pvb])
                    RS["PT"].put(*pts)
                    yield
                    dslot = rr_den[0] % 4
                    rr_den[0] += 1
                    dn = den.ap[:, dslot, 0:4]
                    rdn = den.ap[:, dslot, 4:8]
                    tt("dve", dn, pv[:, :, 64], esink.ap[:, l, 4 * g:4 * g + 4], ALU.add, reads=[pvb, esink], writes=[den])
                    op("dve", lambda e, rdn=rdn, dn=dn: e.reciprocal(out=rdn, in_=dn), writes=[den])
                    tt("dve", ya.ap[:, 4 * gl:4 * gl + 4, :], pv[:, :, 0:64], rdn.unsqueeze(2).to_broadcast([128, 4, 64]),
                       ALU.mult, reads=[pvb, den], writes=[ya])
                    mixp.put(pvb)
                    yield
                (tb,) = yield from take([(tbp, 1)])
                pe_group([trf(tb.bf[:, i * 128:(i + 1) * 128], ya.ap[:, 2 * i:2 * i + 2, :].rearrange("p a b -> p (a b)")) for i in range(4)],
                         reads=[ya, identb], writes=[tb])
                RS["ya"].put(ya)
                if l == 0 and t == 1 and c == 0:
                    debug_dump("ya", [ya], ya.ap, [128, 8, 64])
                yield
                yv = yT[:, 4 * c:4 * c + 4, t * 128:(t + 1) * 128]
                tt("dve", yv, yv, tb.bf.rearrange("p (a b) -> p a b", a=4), ALU.mult,
                   reads=[tb], writes=[yT_b[4 * c + i][t] for i in range(4)])
                tbp.put(tb)
                mark_done("q_done", 2 * NT, "snapX_swa")

            def vs_chain(t, acc, RG, gv):
                act(gv.ap, acc.ap, AF.Gelu_apprx_tanh, reads=[acc], writes=[gv, s1], accum_out=s1.ap[:, t:t + 1])
                yield
                act(junk.ap, gv.ap, AF.Square, reads=[gv], writes=[junk, s2], accum_out=s2.ap[:, t:t + 1])
                mean, msq, var, rs = [gstat.ap[:, t, i:i + 1] for i in range(4)]
                ts("dve", mean, s1.ap[:, t:t + 1], 1.0 / 512, None, ALU.mult, reads=[s1], writes=[gstat])
                tt("dve", msq, mean, mean, ALU.mult, writes=[gstat])
                stt("dve", var, s2.ap[:, t:t + 1], 1.0 / 512, msq, ALU.mult, ALU.subtract, reads=[s2], writes=[gstat])
                yield
                act(var, var, AF.Ln, reads=[smallc], writes=[gstat], bias=epsc, scale=1.0)
                act(rs, var, AF.Exp, writes=[gstat], scale=-0.5)
                ts("dve", gv.ap, gv.ap, mean, rs, ALU.subtract, ALU.mult, reads=[gstat], writes=[gv])
                tt("dve", gv.ap, gv.ap, lng_sb.ap, ALU.mult, reads=[lng_sb], writes=[gv])
                tt("dve", RG["vn"][t].ap, gv.ap, lnb_sb.ap, ALU.add, reads=[gv, lnb_sb], writes=[RG["vn"][t]])
                if l == 0 and t == 1:
                    debug_dump("vn", [RG["vn"][t]], RG["vn"][t].ap, [128, 512])
                RG["gv"].put(gv)
                state["vn", pid, t] = True

            def u_chain(t, acc, RG, gu):
                act(gu.ap, acc.ap, AF.Gelu_apprx_tanh, reads=[acc], writes=[gu])
                yield
                yield from await_keys(("vn", pid, t))
                mx, yb = yield from take([(mixp, 1), (RG["yb"], 1)])
                vn = RG["vn"][t]
                pe_group([mmf(mx.ap[:, g * 64:(g + 1) * 64], sgWm.ap[:, g, :], vn.ap[:, g * 64:(g + 1) * 64]) for g in range(8)],
                         reads=[sgWm, vn], writes=[mx])
                yield
                tmx = RG["tmx"][0]
                tt("dve", tmx.ap, mx.ap.rearrange("p (a b) -> p a b", a=8), sgb_sb.ap[:, l, :].unsqueeze(2).to_broadcast([128, 8, 64]),
                   ALU.add, reads=[mx, sgb_sb], writes=[tmx])
                mixp.put(mx)
                tt("dve", yb.ap, tmx.ap.rearrange("p a b -> p (a b)"), gu.ap, ALU.mult, reads=[tmx, gu], writes=[yb])
                RG["gu"].put(gu)
                if l == 0 and t == 1:
                    debug_dump("yb", [yb], yb.ap, [128, 512])
                yield
                (tb,) = yield from take([(tbp, 1)])
                pe_group([trf(tb.bf[:, i * 128:(i + 1) * 128], yb.ap[:, i * 128:(i + 1) * 128]) for i in range(4)],
                         reads=[yb, identb], writes=[tb])
                RG["yb"].put(yb)
                yield
                yv = yT[:, 8:12, t * 128:(t + 1) * 128]
                tt("dve", yv, yv, tb.bf.rearrange("p (a b) -> p a b", a=4), ALU.mult,
                   reads=[tb], writes=[yT_b[8 + i][t] for i in range(4)])
                tbp.put(tb)
                state["ufin", pid, t] = True
                mark_done("u_done", NT, "snapY_gmlp")

            RS = None
            RG = None
            for name in CHUNKS:
                if name in Z_FC0:
                    z_chunk(name)
                    continue
                if name == "kv":
                    barrier(wait_state(state, "snapX_glat"))
                    RS = region_swa()
                    for t in range(NT):
                        memset("dve", RS["Vb"][t].ap, 1.0, writes=[RS["Vb"][t]])
                    for h2 in range(0 if "setup" in _os.environ.get("KV_SKIP", "") else 2):
                        bb = RS["qtmp"][0]
                        v4 = bb.ap.rearrange("p a b -> p (a b)").rearrange("p (a b) -> p a b", a=4)
                        dma("sp", "d_sgw", v4, sgw_d[l].rearrange("p (a b) -> p a b", a=8)[:, h2 * 4:(h2 + 1) * 4, :], writes=[bb])
                        tt("dve", sgWm.ap[:, h2 * 4:(h2 + 1) * 4, :], v4, trif.unsqueeze(1).to_broadcast([128, 4, 128]),
                           ALU.mult, reads=[bb, cst], writes=[sgWm])
                if name == "vs":
                    barrier(wait_state(state, "snapY_glap"))
                    RG = region_gmlp()
                wbb = wb_consume()
                for t in range(NT):
                    res = None
                    if name == "cqk":
                        (res,) = take_now([(RT["gqk"], 1)])
                    elif name == "kv":
                        (res,) = take_now([(RS["kr"], 1)])
                    elif name in ("qA", "qB"):
                        (res,) = take_now([(RS["qr"], 1)])
                    elif name == "vs":
                        (res,) = take_now([(RG["gv"], 1)])
                    elif name == "u":
                        (res,) = take_now([(RG["gu"], 1)])
                    acc = tok_group(wbb, t, pid)
                    if t == NT - 1:
                        wb_issue()
                    if name == "cv":
                        cp("dve", RT["Vg"][t].ap, acc.ap, reads=[acc], writes=[RT["Vg"][t]])
                        spawn(sp_chain(t))
                    else:
                        if name == "cqk":
                            g = gla_chain(t, acc, res)
                        elif name == "kv":
                            g = kv_chain(t, acc, RS, res)
                            if "corr" not in _os.environ.get("KV_SKIP", ""):
                                spawn(corr_chain(t))
                        elif name in ("qA", "qB"):
                            g = q_chain(0 if name == "qA" else 1, t, acc, RS, res)
                        elif name == "vs":
                            g = vs_chain(t, acc, RG, res)
                        else:
                            g = u_chain(t, acc, RG, res)
                        next(g)
                        spawn(g)
                    step_all()
                if name == "kv" and p == 0 and "modg" not in _os.environ.get("KV_SKIP", ""):
                    mod_compute(l, "g")
                if name == "kv" and p == 1 and l + 1 < L:
                    mod_compute(l + 1, "ss")
                if stop == name:
                    raise StopBuild()

            wait_state(state, "snapX_swa")
            if dbg and l == 0:
                ensure(*[("ufin", pid, t) for t in range(NT)])
                d_ = nc.dram_tensor("dbg_yT", [128, KC, 1024], F32, kind="ExternalOutput").ap()
                dbg_t.append(dma("pool", "d_dbg_yT", d_, yT, reads=[yT_b[fc][t] for fc in range(KC) for t in range(NT)]))

            def xres_load(t):
                b = xres[t % 2]
                w = [x1_written[p * NT + t]] if l > 0 else []
                dma("sp", f"d_xr{t % 2}", b.ap, src_d[p * 1024 + t * 128: p * 1024 + (t + 1) * 128, :], writes=[b], waits=w)
                state["xres", pid, t] = True

            def fin_chain(t, RF):
                xb = xres[t % 2]
                yield from await_keys(("xres", pid, t))
                op("dve", lambda e: e.reduce_sum(out=ossum.ap[:, t:t + 1], in_=oss.ap[:, t, :], axis=AX.X), reads=[oss], writes=[ossum])
                act(rstdO.ap[:, t:t + 1], ossum.ap[:, t:t + 1], AF.Ln, reads=[ossum, smallc], writes=[rstdO], scale=1.0 / D, bias=epsc)
                act(rstdO.ap[:, t:t + 1], rstdO.ap[:, t:t + 1], AF.Exp, writes=[rstdO], scale=-0.5)
                tmp = RF["tmp"][0]
                stt("dve", tmp.ap.rearrange("p (a b) -> p a b", a=KC), hT[:, :, t * 128:(t + 1) * 128], rstdO.ap[:, t:t + 1],
                    GGs[l].ap.rearrange("p (a b) -> p a b", a=KC), ALU.mult, ALU.mult,
                    reads=[hT_b[t], rstdO, GGs[l]], writes=[tmp])
                tt("dve", xb.ap, tmp.ap, xb.ap, ALU.add, reads=[tmp], writes=[xb])
                x1_written[p * NT + t] = dma("sp", f"d_xo{t % 2}", dst_d[p * 1024 + t * 128: p * 1024 + (t + 1) * 128, :], xb.ap, reads=[xb])
                if t + 2 < NT:
                    xres_load(t + 2)
                mark_done("fin_done", NT, "snapXY")

            RF = None
            for n in range(4):
                wbb = wb_consume()
                if n == 3:
                    barrier(wait_state(state, "snapY_gmlp"))
                    RF = region_fin()
                    xres_load(0)
                    xres_load(1)
                for t in range(NT):
                    ensure(("ufin", pid, t))
                    acc = next_acc()
                    items = [mmf(acc.ap, yT[:, kc, t * 128:(t + 1) * 128], wbb.ap[:, kc, :], kc == 0, kc == KC - 1) for kc in range(KC)]
                    pe_group(items, reads=[yT_b[kc][t] for kc in range(KC)] + [wbb], writes=[acc])
                    if t == NT - 1:
                        wb_issue()
                    t_sq = act(junk.ap, acc.ap, AF.Square, reads=[acc], writes=[junk, oss], accum_out=oss.ap[:, t, n:n + 1])
                    cp("dve", hT[:, 4 * n:4 * n + 4, t * 128:(t + 1) * 128], acc.ap.rearrange("p (a b) -> p a b", a=4),
                       reads=[acc], writes=[hT_b[t]], waits=[t_sq])
                    if n == 3:
                        spawn(fin_chain(t, RF))
                    step_all()

        state["setup"] = P.snapshot()
        try:
            mod_compute(0, "ss")
            if stop == "mod":
                raise StopBuild()
            for l in range(L):
                for p in range(2):
                    run_pass(l, p)
        except StopBuild:
            pass
        flush()
        if stop is not None:
            x1_written[0] = dma("sp", "d_fin", y_d[0:128, :], xres[0].ap, reads=[xres[0]])
            fin_t = [P.op(e_, lambda e: e.memset(junk.ap[:, 0:8], 0.0), [P.snapshot()]) for e_ in ("dve",)]
            P.wait_only("sp", fin_t)
        P.wait_only("sp", [x1_written[t] for t in range(2 * NT)] + dbg_t)
        P.wait_only("pool", dbg_t)
        P.emit(block)
    return nc


_NC_CACHE = {}


def _kmajor(w2d):
    n = w2d.shape[1]
    return np.ascontiguousarray(w2d.reshape(KC, 128, n).transpose(1, 0, 2)).reshape(128, KC * n)


def _prep(inputs):
    f32 = np.float32
    x = np.asarray(inputs["x"], f32)
    c = np.asarray(inputs["c"], f32)
    positions = np.asarray(inputs["positions"], np.int32)
    w_mod = np.asarray(inputs["w_mod"], f32)
    b_mod = np.asarray(inputs["b_mod"], f32)
    g_pre = np.asarray(inputs["g_pre"], f32)
    g_post = np.asarray(inputs["g_post"], f32)
    w_in = np.asarray(inputs["w_in"], f32)
    w_out = np.asarray(inputs["w_out"], f32)
    swa_sinks = np.asarray(inputs["swa_sinks"], f32)
    sg_w = np.asarray(inputs["sg_w"], f32)
    sg_b = np.asarray(inputs["sg_b"], f32)
    sg_ln_g = np.asarray(inputs["sg_ln_g"], f32)
    sg_ln_b = np.asarray(inputs["sg_ln_b"], f32)
    gla_w_gate_up = np.asarray(inputs["gla_w_gate_up"], f32)
    gla_b_gate = np.asarray(inputs["gla_b_gate"], f32)
    gla_norm_g = np.asarray(inputs["gla_norm_g"], f32)

    ZO = 3600
    qcolsA = np.concatenate([np.arange(h * 64, (h + 1) * 64) for h in range(0, 8)])
    qcolsB = np.concatenate([np.arange(h * 64, (h + 1) * 64) for h in range(8, 16)])
    chunk_cols = {
        "cv": np.arange(3072, 3584), "cqk": np.arange(2560, 3072),
        "zc": np.arange(ZO + 1536, ZO + 2048), "za0": np.arange(ZO, ZO + 512), "za1": np.arange(ZO + 512, ZO + 1024),
        "kv": np.arange(1024, 1536), "qA": qcolsA, "qB": qcolsB,
        "zb": np.arange(ZO + 1024, ZO + 1536), "vs": np.arange(2048, 2560), "u": np.arange(1536, 2048),
    }
    win = np.empty((2 * 11, 128, KC * 512), f32)
    wing = np.empty((2, 128, KC * 16), f32)
    wout = np.empty((2 * 4, 128, KC * 512), f32)
    for l in range(2):
        for j, name in enumerate(CHUNKS):
            win[l * 11 + j] = _kmajor(w_in[l][:, chunk_cols[name]])
        wing[l] = _kmajor(w_in[l][:, 3584:3600])
        for n in range(4):
            wout[l * 4 + n] = _kmajor(w_out[l][:, n * 512:(n + 1) * 512])
    sgw = np.ascontiguousarray(sg_w.transpose(0, 3, 1, 2)).reshape(2, 128, 1024)
    sgb = np.ascontiguousarray(sg_b.transpose(2, 0, 1)).reshape(128, 16)
    rep = lambda v: np.ascontiguousarray(np.broadcast_to(v.reshape(1, -1), (128, v.size)))
    lng, lnb = rep(sg_ln_g), rep(sg_ln_b)
    wup = np.ascontiguousarray(np.concatenate([gla_w_gate_up, gla_b_gate[:, None, :]], axis=1).transpose(1, 0, 2)).reshape(17, 512)
    gng = rep(gla_norm_g)
    gpreT = np.ascontiguousarray(g_pre.reshape(2, KC, 128).transpose(2, 0, 1)).reshape(128, 32)
    gpost = rep(g_post)
    sinks = rep(swa_sinks)
    k = np.arange(128)[:, None]
    q = np.arange(128)[None, :]
    cst = np.concatenate([
        np.eye(128, dtype=f32), (k > q).astype(f32), (k <= q).astype(f32),
        np.where(k <= q, f32(-1.0 / 16), f32(0)).astype(f32), np.where(k > q, f32(-1.0 / 16), f32(0)).astype(f32)], axis=1)
    inv_freq = (np.float32(500000.0) ** (-(np.arange(8, dtype=f32) * f32(2.0 / 16)))).astype(f32)

    import os as _os1
    if _os1.environ.get("TINY") == "1":
        win = np.ascontiguousarray(win[0:2])
        wout = np.ascontiguousarray(wout[0:1])
    wmod = np.empty((24, 128, KC * 512), f32)
    for l in range(2):
        for ci in range(12):
            wmod[l * 12 + ci] = _kmajor(w_mod[l][:, ci * 512:(ci + 1) * 512])
    bmod = np.ascontiguousarray(b_mod.reshape(2, 1, 6144))
    shared = dict(win=win, wing=wing, wout=wout, sgw=sgw, sgb=sgb, lng=lng, lnb=lnb, wup=wup, gng=gng,
                  gpreT=gpreT, gpost=gpost, sinks=sinks, cst=np.ascontiguousarray(cst), wmod=wmod, bmod=bmod)
    maps = []
    for core in range(4):
        b = core
        m = dict(shared)
        m["x"] = np.ascontiguousarray(x[b])
        m["pos"] = np.ascontiguousarray(positions[b].reshape(16, 128).T)
        m["cT"] = np.ascontiguousarray(c[b].reshape(KC, 128).T)
        cst2 = np.zeros((128, 16), f32)
        cst2[:, 8:16] = inv_freq[None, :]
        m["cst2"] = cst2
        maps.append(m)
    return maps


def kernel(**inputs):
    if "nc" not in _NC_CACHE:
        _NC_CACHE["nc"] = build(2)
    nc = _NC_CACHE["nc"]
    maps = _prep(inputs)
    res = run_bass_kernel_spmd(nc, maps, core_ids=list(range(4)))
    out = np.empty((4, 2048, D), np.float32)
    for core in range(4):
        out[core] = np.asarray(res.results[core]["y"], np.float32)
    return out
```
